# Optimizing a Trainium2 kernel written in Bass

```python
import jax, jax.numpy as jnp
from jax import lax
import numpy as np

D_MODEL = 1024
BATCH = 4
SEQ = 8192
DEPTH = 4

MEM_LEN = 256
GLA_HEADS = 4
GLA_DK = 64
GLA_DV = 128
GLA_RANK = 16
GLA_TAU = 16.0
GLA_CHUNK = 64
GLA_QK = GLA_HEADS * GLA_DK
GLA_V = GLA_HEADS * GLA_DV
FOX_HEADS = 4
FOX_HD = 128
FOX_W = FOX_HEADS * FOX_HD
FOX_BLOCK = 128
MEM_HEADS = 4
MEM_HD = 128
MEM_W = MEM_HEADS * MEM_HD
N_BRANCH = 3
BRANCH_W = 512
D_FF = -(-8 * D_MODEL // (3 * 256)) * 256
EPS = 1e-6

IN_SIZES = (GLA_QK, GLA_QK, GLA_V, GLA_V, GLA_RANK,
            FOX_W, FOX_W, FOX_W, FOX_HEADS,
            MEM_W,
            N_BRANCH * D_MODEL)
IN_WIDTH = sum(IN_SIZES)
IN_OFFSETS = tuple(int(o) for o in np.cumsum(IN_SIZES)[:-1])

kernel_name = "hybrid_gla_fox_mem_parallel_trunk"


def _rmsnorm(x, g):
    xf = x.astype(jnp.float32)
    y = xf * lax.rsqrt(jnp.mean(xf * xf, axis=-1, keepdims=True) + EPS)
    return (y * g.astype(jnp.float32)).astype(x.dtype)


def _heads(t, n):
    b, s, w = t.shape
    return t.reshape(b, s, n, w // n).transpose(0, 2, 1, 3)


def _merge(t):
    b, h, s, d = t.shape
    return t.transpose(0, 2, 1, 3).reshape(b, s, h * d)


def _gla(q, k, v, log_a):
    out_dtype = v.dtype
    b, h, s, dk = q.shape
    dv = v.shape[-1]
    c = GLA_CHUNK
    n = s // c
    qc = (q.astype(jnp.float32) * (GLA_DK ** -0.5)).reshape(b, h, n, c, dk)
    kc = k.astype(jnp.float32).reshape(b, h, n, c, dk)
    vc = v.astype(jnp.float32).reshape(b, h, n, c, dv)
    cum = jnp.cumsum(log_a.astype(jnp.float32).reshape(b, h, n, c, dk), axis=-2)
    cum_last = cum[..., -1:, :]
    q_in = qc * jnp.exp(cum)
    k_in = kc * jnp.exp(-cum)
    k_out = kc * jnp.exp(cum_last - cum)
    tril = jnp.tril(jnp.ones((c, c), dtype=bool))
    attn = jnp.where(tril, jnp.einsum('bhncd,bhnsd->bhncs', q_in, k_in), 0.0)
    o_intra = jnp.einsum('bhncs,bhnsv->bhncv', attn, vc)
    chunk_state = jnp.einsum('bhncd,bhncv->bhndv', k_out, vc)
    decay = jnp.exp(cum_last[..., 0, :])

    def step(state, inp):
        d, cs = inp
        return d[..., None] * state + cs, state

    _, s_in = lax.scan(step, jnp.zeros((b, h, dk, dv), jnp.float32),
                       (jnp.moveaxis(decay, 2, 0), jnp.moveaxis(chunk_state, 2, 0)))
    s_in = jnp.moveaxis(s_in, 0, 2)
    o_inter = jnp.einsum('bhncd,bhndv->bhncv', q_in, s_in)
    return (o_intra + o_inter).reshape(b, h, s, dv).astype(out_dtype)


def _fox(q, k, v, log_f):
    b, h, s, hd = q.shape
    nb = s // FOX_BLOCK
    scale = hd ** -0.5
    cum = jnp.cumsum(log_f, axis=-1)
    qb = jnp.moveaxis(q.reshape(b, h, nb, FOX_BLOCK, hd), 2, 0)
    cb = jnp.moveaxis(cum.reshape(b, h, nb, FOX_BLOCK), 2, 0)
    starts = jnp.arange(nb, dtype=jnp.int32) * FOX_BLOCK
    kpos = jnp.arange(s, dtype=jnp.int32)

    def block(args):
        q_i, c_i, i0 = args
        logits = (jnp.einsum('bhqd,bhkd->bhqk', q_i, k).astype(jnp.float32) * scale
                  + c_i[..., None] - cum[:, :, None, :])
        qpos = i0 + jnp.arange(FOX_BLOCK, dtype=jnp.int32)
        mask = kpos[None, :] <= qpos[:, None]
        p = jax.nn.softmax(jnp.where(mask, logits, -jnp.inf), axis=-1)
        return jnp.einsum('bhqk,bhkd->bhqd', p.astype(v.dtype), v)

    out = lax.map(block, (qb, cb, starts))
    return jnp.moveaxis(out, 0, 2).reshape(b, h, s, hd)


def _cross(q, k, v):
    logits = jnp.einsum('bhqd,bhkd->bhqk', q, k).astype(jnp.float32) * (q.shape[-1] ** -0.5)
    p = jax.nn.softmax(logits, axis=-1)
    return jnp.einsum('bhqk,bhkd->bhqd', p.astype(v.dtype), v)


def setup_inputs(seed: int = 0) -> dict:
    key = jax.random.key(seed)
    ks = jax.random.split(key, 24)
    L, D = DEPTH, D_MODEL

    def nrm(k, shape, scale):
        return jax.random.normal(k, shape, jnp.float32) * scale

    def gain(k, shape):
        return 1.0 + 0.02 * jax.random.normal(k, shape, jnp.float32)

    return {
        "x": nrm(ks[0], (BATCH, SEQ, D), 1.0),
        "mem": nrm(ks[1], (BATCH, MEM_LEN, D), 1.0),
        "g_mix": gain(ks[2], (L, D)),
        "w_in": nrm(ks[3], (L, D, IN_WIDTH), D ** -0.5),
        "w_gla_a2": nrm(ks[4], (L, GLA_RANK, GLA_QK), GLA_RANK ** -0.5),
        "b_gla_a": nrm(ks[5], (L, GLA_QK), 0.1),
        "g_gla_out": gain(ks[6], (L, GLA_V)),
        "b_fox_f": nrm(ks[7], (L, FOX_HEADS), 0.1),
        "g_fox_q": gain(ks[8], (L, FOX_HD)),
        "g_fox_k": gain(ks[9], (L, FOX_HD)),
        "g_mem": gain(ks[10], (L, D)),
        "w_mem_kv": nrm(ks[11], (L, D, 2 * MEM_W), D ** -0.5),
        "g_mem_q": gain(ks[12], (L, MEM_HD)),
        "g_mem_k": gain(ks[13], (L, MEM_HD)),
        "b_gate": nrm(ks[14], (L, N_BRANCH * D), 0.1),
        "w_branch": nrm(ks[15], (L, N_BRANCH, BRANCH_W, D), BRANCH_W ** -0.5),
        "w_out": nrm(ks[16], (L, D, D), D ** -0.5),
        "g_ffn": gain(ks[17], (L, D)),
        "w_ffn_gate": nrm(ks[18], (L, D, D_FF), D ** -0.5),
        "w_ffn_up": nrm(ks[19], (L, D, D_FF), D ** -0.5),
        "w_ffn_down": nrm(ks[20], (L, D_FF, D), D_FF ** -0.5),
    }


def reference(x, mem, g_mix, w_in, w_gla_a2, b_gla_a, g_gla_out, b_fox_f, g_fox_q, g_fox_k,
              g_mem, w_mem_kv, g_mem_q, g_mem_k, b_gate, w_branch, w_out,
              g_ffn, w_ffn_gate, w_ffn_up, w_ffn_down):
    for l in range(DEPTH):
        h = _rmsnorm(x, g_mix[l])
        proj = h @ w_in[l]
        (gq, gk, gv, gg, ga1, fq, fk, fv, ff, mq, bg) = jnp.split(proj, IN_OFFSETS, axis=-1)

        log_a = jax.nn.log_sigmoid((ga1 @ w_gla_a2[l] + b_gla_a[l]).astype(jnp.float32)) / GLA_TAU
        o_gla = _gla(_heads(gq, GLA_HEADS), _heads(gk, GLA_HEADS), _heads(gv, GLA_HEADS),
                     _heads(log_a, GLA_HEADS))
        o_gla = _rmsnorm(o_gla, g_gla_out[l].reshape(GLA_HEADS, 1, GLA_DV))
        y_gla = _merge(o_gla) * jax.nn.silu(gg)

        log_f = jax.nn.log_sigmoid((ff + b_fox_f[l]).astype(jnp.float32)).transpose(0, 2, 1)
        o_fox = _fox(_rmsnorm(_heads(fq, FOX_HEADS), g_fox_q[l]),
                     _rmsnorm(_heads(fk, FOX_HEADS), g_fox_k[l]),
                     _heads(fv, FOX_HEADS), log_f)
        y_fox = _merge(o_fox)

        mkv = _rmsnorm(mem, g_mem[l]) @ w_mem_kv[l]
        mk, mv = jnp.split(mkv, 2, axis=-1)
        o_mem = _cross(_rmsnorm(_heads(mq, MEM_HEADS), g_mem_q[l]),
                       _rmsnorm(_heads(mk, MEM_HEADS), g_mem_k[l]),
                       _heads(mv, MEM_HEADS))
        y_mem = _merge(o_mem)

        gates = jax.nn.sigmoid(bg + b_gate[l])
        merged = None
        for i, y_b in enumerate((y_gla, y_fox, y_mem)):
            term = gates[..., i * D_MODEL:(i + 1) * D_MODEL] * (y_b @ w_branch[l, i])
            merged = term if merged is None else merged + term
        x = x + merged @ w_out[l]

        h2 = _rmsnorm(x, g_ffn[l])
        x = x + (jax.nn.silu(h2 @ w_ffn_gate[l]) * (h2 @ w_ffn_up[l])) @ w_ffn_down[l]
    return x
```

```python
import numpy as np
import concourse.bass as bass
import concourse.mybir as mybir
from concourse.bass_utils import run_bass_kernel_spmd
from contextlib import ExitStack

F32 = mybir.dt.float32
BF16 = mybir.dt.bfloat16
AF = mybir.ActivationFunctionType
ALU = mybir.AluOpType

ENGS = ("pe", "act", "dve", "pool", "sp")
EPOCH = 30000
N_DMA_SEMS = 64

D = 1024
KC = 8
TT = 512
DFF = 2816
NF = DFF // 128
INW = 6676
WA_COLS = 3092
WC_COLS = INW - WA_COLS
O_GQ, O_GK, O_GV, O_GG, O_GA, O_FQ, O_FK, O_FV, O_FF = 0, 256, 512, 1024, 1536, 1552, 2064, 2576, 3088
MEM = 256
EPS = 1e-6
NP_ = 56
NEG = -30000.0


class Buf:
    __slots__ = ("name", "last_w", "readers", "const", "psum")
    registry = []

    def __init__(self, name, psum=False):
        self.name = name
        self.last_w = None
        self.readers = []
        self.const = False
        self.psum = psum
        Buf.registry.append(self)


class Op:
    __slots__ = ("eng", "emit", "deps", "needs_inc", "seq", "is_dma", "dsem", "dval", "idx", "extra_waits")


class Prog:
    def __init__(self, nc, es):
        self.nc = nc
        self.es = es
        self.ops = {e: [] for e in ENGS}
        self.dma_sems = [es.enter_context(nc.semaphore(f"dq{i}")) for i in range(N_DMA_SEMS)]
        self.dma_cnt = [0] * N_DMA_SEMS
        self.dma_last = [None] * N_DMA_SEMS
        self.dma_i = 0
        self.eng_sems = {}
        self.pending_dma = []

    def _new(self, eng):
        o = Op()
        o.eng = eng
        o.emit = None
        o.needs_inc = False
        o.seq = None
        o.is_dma = False
        o.dsem = None
        o.dval = 0
        o.extra_waits = []
        o.deps = []
        return o

    def _record(self, o, reads, writes):
        deps = set()
        for b in reads:
            if b.last_w is not None:
                deps.add(b.last_w)
            if b.psum:
                for r_ in b.readers:
                    if r_.eng != o.eng:
                        deps.add(r_)
        for b in writes:
            if b.last_w is not None:
                deps.add(b.last_w)
            deps.update(b.readers)
        lst = self.ops[o.eng]
        o.idx = len(lst)
        fdeps = []
        for d in deps:
            if d is o:
                continue
            if not d.is_dma and not o.is_dma and d.eng == o.eng:
                if o.eng == "pe":
                    continue
                if o.idx - d.idx > 2:
                    continue
            fdeps.append(d)
        o.deps = fdeps
        for d in fdeps:
            d.needs_inc = True
        for b in writes:
            b.last_w = o
            b.readers = []
        for b in reads:
            if b.const:
                continue
            if all(b is not w for w in writes):
                b.readers.append(o)
        lst.append(o)
        return o

    def op(self, eng, emit, reads=(), writes=()):
        o = self._new(eng)
        o.emit = emit
        return self._record(o, reads, writes)

    def dma(self, queue, out, in_, reads=(), writes=()):
        o = self._new(queue)
        o.is_dma = True
        i = self.dma_i % N_DMA_SEMS
        self.dma_i += 1
        if self.dma_last[i] is not None:
            o.extra_waits.append(self.dma_last[i])
        self.dma_cnt[i] += 16
        o.dsem = self.dma_sems[i]
        o.dval = self.dma_cnt[i]
        self.dma_last[i] = o
        o.emit = lambda e, out=out, in_=in_: e.dma_start(out=out, in_=in_)
        self.pending_dma.append(o)
        return self._record(o, reads, writes)

    def barrier(self, engines=ENGS):
        lasts = []
        for e in ENGS:
            for o in reversed(self.ops[e]):
                if not o.is_dma and o.emit is not None:
                    lasts.append(o)
                    break
        latest = {}
        for d in self.pending_dma:
            latest[d.dsem.num] = d
        dm = list(latest.values())
        self.pending_dma = []
        for e in engines:
            o = self._new(e)
            o.idx = len(self.ops[e])
            o.deps = list(lasts) + dm
            for d in o.deps:
                d.needs_inc = True
            self.ops[e].append(o)
        for b in Buf.registry:
            b.last_w = None
            b.readers = []

    def finalize(self):
        nc = self.nc
        for e in ENGS:
            n = 0
            for o in self.ops[e]:
                if o.needs_inc and not o.is_dma and o.emit is not None:
                    n += 1
                    o.seq = n
            nep = (n + EPOCH - 1) // EPOCH
            self.eng_sems[e] = [self.es.enter_context(nc.semaphore(f"es_{e}{k}")) for k in range(max(nep, 1))]

        def handle(d):
            if d.is_dma:
                return d.dsem, d.dval
            k = (d.seq - 1) // EPOCH
            return self.eng_sems[d.eng][k], d.seq - k * EPOCH

        engobj = {"pe": "tensor", "act": "scalar", "dve": "vector", "pool": "gpsimd", "sp": "sync"}
        with nc.Block() as block:
            def make(ename):
                def body(eng):
                    waited = {}
                    for o in self.ops[ename]:
                        for d in list(o.deps) + list(o.extra_waits):
                            s, v = handle(d)
                            if waited.get(s.num, 0) >= v:
                                continue
                            waited[s.num] = v
                            eng.wait_ge(s, v)
                        if o.emit is None:
                            continue
                        ins = o.emit(eng)
                        if o.is_dma:
                            ins.then_inc(o.dsem, 16)
                        elif o.needs_inc:
                            s, v = handle(o)
                            ins.then_inc(s, 1)
                return body
            for ename in ENGS:
                if self.ops[ename]:
                    getattr(block, engobj[ename])(make(ename))


def build_program(S, depth, dbg=False):
    NT = S // TT
    NB = S // 128
    nc = bass.Bass("TRN2", target_bir_lowering=False)
    Buf.registry = []

    def din(name, shape, dt=F32):
        return nc.dram_tensor(name, list(shape), dt, kind="ExternalInput").ap()

    def dint(name, shape, dt):
        return nc.dram_tensor(name, list(shape), dt, kind="Internal").ap()

    x_in = din("x", [S, D])
    mem_in = din("mem", [MEM, D])
    w_in = din("w_in", [depth, D, INW])
    w_mkv = din("w_mem_kv", [depth, D, 2 * 512])
    w_br = din("w_branch", [depth, 3 * 512, D])
    w_out = din("w_out", [depth, D, D])
    w_fg = din("w_ffn_gate", [depth, D, DFF])
    w_fu = din("w_ffn_up", [depth, D, DFF])
    w_fd = din("w_ffn_down", [depth, DFF, D])
    w_a2 = din("w_gla_a2", [depth, 16, 256])
    pp_in = din("pp", [depth, 128, NP_])
    pb_in = din("pb", [depth, 128, 260])
    consts_in = din("consts", [128, 4 * 128])
    out_d = nc.dram_tensor("out", [S, D], F32, kind="ExternalOutput").ap()

    b_in = dint("b_in", [depth, D, INW], BF16)
    b_mkv = dint("b_mkv", [depth, D, 1024], BF16)
    b_br = dint("b_br", [depth, 1536, D], BF16)
    b_out = dint("b_out", [depth, D, D], BF16)
    b_fg = dint("b_fg", [depth, D, DFF], BF16)
    b_fu = dint("b_fu", [depth, D, DFF], BF16)
    b_fd = dint("b_fd", [depth, DFF, D], BF16)
    xT_d = dint("xT", [D, S], F32)
    foxq_d = dint("foxq", [4, 128, S], BF16)
    foxk_d = dint("foxk", [4, 128, S], BF16)
    foxv_d = dint("foxv", [S, 512], BF16)
    yT_d = dint("yT", [D, S], BF16)
    dbg_d = {}
    if dbg:
        dbg_d["xT0"] = nc.dram_tensor("dbg_xT0", [D, S], F32, kind="ExternalOutput").ap()
        dbg_d["yT"] = nc.dram_tensor("dbg_yT", [D, S], BF16, kind="ExternalOutput").ap()
        dbg_d["foxq"] = nc.dram_tensor("dbg_foxq", [4, 128, S], BF16, kind="ExternalOutput").ap()
        dbg_d["foxk"] = nc.dram_tensor("dbg_foxk", [4, 128, S], BF16, kind="ExternalOutput").ap()
        dbg_d["foxv"] = nc.dram_tensor("dbg_foxv", [S, 512], BF16, kind="ExternalOutput").ap()
        dbg_d["cum"] = nc.dram_tensor("dbg_cum", [128, NB * 4], F32, kind="ExternalOutput").ap()
        dbg_d["xmid"] = nc.dram_tensor("dbg_xmid", [D, S], F32, kind="ExternalOutput").ap()
        dbg_d["cq"] = nc.dram_tensor("dbg_cq", [128, 512], F32, kind="ExternalOutput").ap()
        dbg_d["tb"] = nc.dram_tensor("dbg_tb", [128, 512], F32, kind="ExternalOutput").ap()
        dbg_d["pt"] = nc.dram_tensor("dbg_pt", [128, 512], BF16, kind="ExternalOutput").ap()
        dbg_d["rl"] = nc.dram_tensor("dbg_rl", [128, 512], F32, kind="ExternalOutput").ap()
        dbg_d["lraw"] = nc.dram_tensor("dbg_lraw", [128, 512], F32, kind="ExternalOutput").ap()
        dbg_d["oraw"] = nc.dram_tensor("dbg_oraw", [128, 512], F32, kind="ExternalOutput").ap()

    with ExitStack() as es:
        P = Prog(nc, es)
        ARENA_F32 = 52736
        arena_t = es.enter_context(nc.sbuf_tensor("arena", [128, ARENA_F32], F32))
        psum_t = es.enter_context(nc.psum_tensor("psum", [128, 8, 512], F32))
        A = arena_t[:, :]
        PSA = psum_t[:, :, :]
        PS = [PSA[:, i, :] for i in range(8)]
        PSB = [Buf(f"ps{i}", psum=True) for i in range(8)]
        off = [0]

        def carve(nelem, dt=F32):
            n32 = nelem if dt == F32 else (nelem + 1) // 2
            assert off[0] + n32 <= ARENA_F32, f"SBUF arena overflow {off[0] + n32}"
            a = A[:, off[0]:off[0] + n32]
            off[0] += n32
            if dt != F32:
                a = a.bitcast(dt)
            return a

        ring = {"i": 0, "lo": 0, "hi": 8}

        def psb():
            n = ring["hi"] - ring["lo"]
            i = ring["lo"] + ring["i"] % n
            ring["i"] += 1
            return i

        cst = carve(512)
        ident = cst[:, 0:128]
        tri = cst[:, 128:256]
        tri16 = cst[:, 256:384]
        maskneg = cst[:, 384:512]
        Bcst = Buf("cst")
        ones_f = carve(128)
        ones_b = carve(128, BF16)
        Bones = Buf("ones")
        pp = carve(NP_)
        pb = carve(260)
        Bpp = Buf("pp")
        wa2f = carve(256)
        wa2b = carve(256, BF16)
        Bwa2 = Buf("wa2")
        memT = carve(KC * MEM).rearrange("p (c n) -> p c n", c=KC)
        BmemT = Buf("memT")
        logf = carve(NB * 4)
        Blogf = Buf("logf")
        cum_all = carve(NB * 4)
        negcum = carve(NB * 4)
        Bcum = Buf("cum")
        Sst = [carve(128) for _ in range(4)]
        Sbf = [carve(128, BF16) for _ in range(4)]
        BS = [Buf(f"S{h}") for h in range(4)]
        BSb = [Buf(f"Sb{h}") for h in range(4)]
        mkn = carve(4 * MEM, BF16).rearrange("p (h n) -> p h n", h=4)
        mv = carve(2 * 512, BF16).rearrange("p (b n) -> p b n", b=2)
        Bmkn = Buf("mkn")
        Bmv = Buf("mv")
        persist_mark = off[0]

        P.dma("sp", cst, consts_in, writes=[Bcst])
        P.op("dve", lambda e: e.memset(ones_f, 1.0), writes=[Bones])
        P.op("dve", lambda e: e.memset(ones_b, 1.0), writes=[Bones])

        pieces = {}
        for l in range(depth):
            for name, src_, dst_, ne in (("in", w_in[l], b_in[l], D * INW), ("mkv", w_mkv[l], b_mkv[l], D * 1024),
                                         ("br", w_br[l], b_br[l], 1536 * D), ("out", w_out[l], b_out[l], D * D),
                                         ("fg", w_fg[l], b_fg[l], D * DFF), ("fu", w_fu[l], b_fu[l], D * DFF),
                                         ("fd", w_fd[l], b_fd[l], DFF * D)):
                rows = ne // 1024
                s2 = src_.rearrange("a b -> (a b)").rearrange("(r c) -> r c", c=1024)
                d2 = dst_.rearrange("a b -> (a b)").rearrange("(r c) -> r c", c=1024)
                lst = []
                r = 0
                while r < rows:
                    n = min(2048, rows - r)
                    lst.append(P.dma("pool", d2[r:r + n, :], s2[r:r + n, :]))
                    r += n
                pieces[(l, name)] = lst
        P.pending_dma = [o for o in P.pending_dma if o.eng != "pool"]

        def wait_weight(l, name, queue="sp"):
            o = P._new(queue)
            o.idx = len(P.ops[queue])
            o.deps = list(pieces[(l, name)])
            P.ops[queue].append(o)

        def mm(bank, out_ap, lhsT, rhs, start, stop, reads):
            P.op("pe", lambda e: e.matmul(out_ap, lhsT=lhsT, rhs=rhs, start=start, stop=stop),
                 reads=reads, writes=[PSB[bank]])

        def act(out, in_, func, reads, writes, bias=0.0, scale=1.0):
            P.op("act", lambda e: e.activation(out=out, in_=in_, func=func, bias=bias, scale=scale),
                 reads=reads, writes=writes)

        def stt(out, in0, scalar, in1, op0, op1, reads, writes, eng="dve"):
            P.op(eng, lambda e: e.scalar_tensor_tensor(out=out, in0=in0, scalar=scalar, in1=in1, op0=op0, op1=op1),
                 reads=reads, writes=writes)

        def tt(eng, out, in0, in1, op, reads, writes):
            P.op(eng, lambda e: e.tensor_tensor(out=out, in0=in0, in1=in1, op=op), reads=reads, writes=writes)

        def cp(eng, out, in_, reads, writes):
            P.op(eng, lambda e: e.tensor_copy(out=out, in_=in_), reads=reads, writes=writes)

        def rstd1(bank_in, n, inv_dim, sq1, Bsq1, rstd, Brstd):
            act(sq1[:, 0:n], PS[bank_in][:, 0:n], AF.Square, [PSB[bank_in]], [Bsq1])
            bk = psb()
            mm(bk, PS[bk][:, 0:n], ones_b, sq1[:, 0:n], True, True, [Bones, Bsq1])
            act(rstd[:, 0:n], PS[bk][:, 0:n], AF.Ln, [PSB[bk]], [Brstd], bias=EPS, scale=inv_dim)
            act(rstd[:, 0:n], rstd[:, 0:n], AF.Exp, [Brstd], [Brstd], scale=-0.5)

        def norm_tile(xt, Bxt, gcol0, n, hT, BhT, sqr, Bsqr, rstd, Brstd):
            bk = psb()
            for c in range(KC):
                s_, bs_ = sqr[c % 2], Bsqr[c % 2]
                act(s_[:, 0:n], xt[:, c, :], AF.Square, [Bxt], [bs_])
                mm(bk, PS[bk][:, 0:n], ones_b, s_[:, 0:n], c == 0, c == KC - 1, [Bones, bs_])
            act(rstd[:, 0:n], PS[bk][:, 0:n], AF.Ln, [PSB[bk]], [Brstd], bias=EPS, scale=1.0 / D)
            act(rstd[:, 0:n], rstd[:, 0:n], AF.Exp, [Brstd], [Brstd], scale=-0.5)
            for c in range(KC):
                stt(hT[:, c, :], xt[:, c, :], pp[:, gcol0 + c:gcol0 + c + 1], rstd[:, 0:n], ALU.mult, ALU.mult,
                    [Bxt, Brstd, Bpp], [BhT])

        xs2 = [carve(4 * D).rearrange("p (j n) -> p j n", j=4) for _ in range(2)]
        Bxs2 = [Buf("xs0"), Buf("xs1")]
        xo2 = [carve(KC * TT).rearrange("p (c n) -> p c n", c=KC) for _ in range(2)]
        Bxo2 = [Buf("xo0"), Buf("xo1")]
        xTv = xT_d.rearrange("(c p) s -> p c s", p=128)
        for t in range(NT):
            xs, Bxs, xo, Bxo = xs2[t % 2], Bxs2[t % 2], xo2[t % 2], Bxo2[t % 2]
            P.dma("sp", xs, x_in[t * TT:(t + 1) * TT, :].rearrange("(j p) n -> p j n", p=128), writes=[Bxs])
            for c in range(KC):
                bk = psb()
                for j in range(4):
                    P.op("pe", lambda e, bk=bk, j=j, c=c, xs=xs: e.transpose(PS[bk][:, j * 128:(j + 1) * 128], xs[:, j, c * 128:(c + 1) * 128], ident),
                         reads=[Bxs, Bcst], writes=[PSB[bk]])
                if c % 2 == 0:
                    cp("dve", xo[:, c, :], PS[bk], [PSB[bk]], [Bxo])
                else:
                    act(xo[:, c, :], PS[bk], AF.Copy, [PSB[bk]], [Bxo])
            P.dma("sp", xTv[:, :, t * TT:(t + 1) * TT], xo, reads=[Bxo])
            if dbg:
                P.dma("sp", dbg_d["xT0"].rearrange("(c p) s -> p c s", p=128)[:, :, t * TT:(t + 1) * TT], xo, reads=[Bxo])
        xs = xs2[0]
        P.dma("sp", xs[:, 0:2, :], mem_in.rearrange("(j p) n -> p j n", p=128), writes=[Bxs2[0]])
        for c in range(KC):
            bk = psb()
            for j in range(2):
                P.op("pe", lambda e, bk=bk, j=j, c=c: e.transpose(PS[bk][:, j * 128:(j + 1) * 128], xs[:, j, c * 128:(c + 1) * 128], ident),
                     reads=[Bxs2[0], Bcst], writes=[PSB[bk]])
            cp("dve", memT[:, c, :], PS[bk][:, 0:256], [PSB[bk]], [BmemT])
        P.barrier()

        for l in range(depth):
            last_layer = (l == depth - 1)
            P.dma("sp", pp, pp_in[l], writes=[Bpp])
            P.dma("sp", pb, pb_in[l], writes=[Bpp])
            P.dma("sp", wa2f[0:16, :], w_a2[l], writes=[Bwa2])
            cp("dve", wa2b[0:16, :], wa2f[0:16, :], [Bwa2], [Bwa2])
            for h in range(4):
                P.op("dve", lambda e, h=h: e.memset(Sst[h][0:64, :], 0.0), writes=[BS[h]])
                P.op("dve", lambda e, h=h: e.memset(Sbf[h][0:64, :], 0.0), writes=[BSb[h]])

            off[0] = persist_mark
            wm = carve(KC * 1024, BF16).rearrange("p (c n) -> p c n", c=KC)
            Bwm = Buf("wm")
            wait_weight(l, "mkv")
            for c in range(KC):
                P.dma("sp", wm[:, c, :], b_mkv[l][c * 128:(c + 1) * 128, :], writes=[Bwm])
            sqr = [carve(512, BF16) for _ in range(2)]; Bsqr = [Buf("sqr0"), Buf("sqr1")]
            mn = carve(KC * MEM, BF16).rearrange("p (c n) -> p c n", c=KC); Bmn = Buf("mn")
            rs = carve(512); Brs = Buf("rs")
            sq1 = carve(512, BF16); Bsq1 = Buf("sq1")
            norm_tile(memT, BmemT, 16, MEM, mn, Bmn, sqr, Bsqr, rs, Brs)
            for h in range(4):
                bk = psb()
                for c in range(KC):
                    mm(bk, PS[bk][:, 0:MEM], wm[:, c, h * 128:(h + 1) * 128], mn[:, c, :], c == 0, c == KC - 1, [Bwm, Bmn])
                rstd1(bk, MEM, 1.0 / 128, sq1, Bsq1, rs, Brs)
                stt(mkn[:, h, :], PS[bk][:, 0:MEM], pp[:, 55:56], rs[:, 0:MEM], ALU.mult, ALU.mult, [PSB[bk], Brs, Bpp], [Bmkn])
            for j in range(2):
                bk = psb()
                for c in range(KC):
                    mm(bk, PS[bk], mn[:, c, j * 128:(j + 1) * 128], wm[:, c, 512:1024], c == 0, c == KC - 1, [Bwm, Bmn])
                act(mv[:, j, :], PS[bk], AF.Copy, [PSB[bk]], [Bmv])
            P.barrier()

            off[0] = persist_mark
            wA = carve(KC * WA_COLS, BF16).rearrange("p (c n) -> p c n", c=KC)
            BwA = Buf("wA")
            wait_weight(l, "in")
            for c in range(KC):
                P.dma("sp", wA[:, c, :], b_in[l][c * 128:(c + 1) * 128, 0:WA_COLS], writes=[BwA])
            xt = carve(KC * TT).rearrange("p (c n) -> p c n", c=KC); Bxt = Buf("xt")
            sqr = [carve(512, BF16) for _ in range(2)]; Bsqr = [Buf("sqr0"), Buf("sqr1")]
            hT = carve(KC * TT, BF16).rearrange("p (c n) -> p c n", c=KC); BhT = Buf("hT")
            rs = carve(512); Brs = Buf("rs")
            rs2 = carve(512); Brs2 = Buf("rs2")
            sq1 = carve(512, BF16); Bsq1 = Buf("sq1")
            qT = [carve(512) for _ in range(4)]; BqT = [Buf(f"qT{h}") for h in range(4)]
            kT = [carve(512) for _ in range(4)]; BkT = [Buf(f"kT{h}") for h in range(4)]
            a1T = carve(512, BF16); Ba1 = Buf("a1T")
            ktm = carve(4 * 256).rearrange("p (j n) -> p j n", j=4); Bktm = Buf("ktm")
            vtm = carve(4 * 512, BF16).rearrange("p (j n) -> p j n", j=4); Bvtm = Buf("vtm")
            la = carve(4 * 256).rearrange("p (j n) -> p j n", j=4); Bla = Buf("la")
            t1 = carve(256); t2 = carve(256); Bt1 = Buf("t1"); Bt2 = Buf("t2")
            ektm = carve(256); Bektm = Buf("ektm")
            kintm = carve(4 * 256, BF16).rearrange("p (j n) -> p j n", j=4); Bkintm = Buf("kintm")
            sgg = [carve(512) for _ in range(4)]; Bsgg = [Buf(f"sgg{h}") for h in range(4)]
            eg = carve(512); Beg = Buf("eg")
            eq = carve(512); ekT = carve(512); Beq = Buf("eq"); BekT = Buf("ekT")
            qin = carve(512, BF16); kin = carve(512, BF16); Bqin = Buf("qin"); Bkin = Buf("kin")
            atb = [carve(128, BF16) for _ in range(2)]; Batb = [Buf("atb0"), Buf("atb1")]
            cst_ = carve(128); Bcs = Buf("cs")
            y1 = carve(512); By1 = Buf("y1")
            ystage = carve(4 * 512, BF16).rearrange("p (h n) -> p h n", h=4); Bys = Buf("ys")
            qstage = carve(4 * 512, BF16).rearrange("p (h n) -> p h n", h=4); Bqs = Buf("qs")
            kstage = carve(4 * 512, BF16).rearrange("p (h n) -> p h n", h=4); Bks = Buf("ks")
            vstage = carve(4 * 512, BF16).rearrange("p (j n) -> p j n", j=4); Bvs = Buf("vs")
            lf1 = carve(16); lf2 = carve(16); Blf = Buf("lf")

            P.dma("sp", xt, xTv[:, :, 0:TT], writes=[Bxt])
            ring["lo"], ring["hi"], ring["i"] = 0, 6, 0
            for t in range(NT):
                norm_tile(xt, Bxt, 0, TT, hT, BhT, sqr, Bsqr, rs, Brs)
                if t + 1 < NT:
                    P.dma("sp", xt, xTv[:, :, (t + 1) * TT:(t + 2) * TT], writes=[Bxt])

                def proj_fm(col0, M):
                    bk = psb()
                    for c in range(KC):
                        mm(bk, PS[bk][0:M, :], wA[:, c, col0:col0 + M], hT[:, c, :], c == 0, c == KC - 1, [BwA, BhT])
                    return bk

                def proj_tm(col0, N, j):
                    bk = psb()
                    for c in range(KC):
                        mm(bk, PS[bk][:, 0:N], hT[:, c, j * 128:(j + 1) * 128], wA[:, c, col0:col0 + N], c == 0, c == KC - 1, [BwA, BhT])
                    return bk

                for h in range(4):
                    bk = proj_fm(O_GQ + h * 64, 64)
                    act(qT[h][0:64, :], PS[bk][0:64, :], AF.Copy, [PSB[bk]], [BqT[h]], scale=0.125)
                    bk = proj_fm(O_GK + h * 64, 64)
                    cp("dve", kT[h][0:64, :], PS[bk][0:64, :], [PSB[bk]], [BkT[h]])
                bk = proj_fm(O_GA, 16)
                act(a1T[0:16, :], PS[bk][0:16, :], AF.Copy, [PSB[bk]], [Ba1])
                for h in range(4):
                    bk = proj_fm(O_GG + h * 128, 128)
                    act(eg, PS[bk], AF.Exp, [PSB[bk]], [Beg], scale=-1.0)
                    P.op("pool", lambda e, eg=eg: e.tensor_scalar_add(out=eg, in0=eg, scalar1=1.0), reads=[Beg], writes=[Beg])
                    P.op("dve", lambda e, eg=eg: e.reciprocal(out=eg, in_=eg), reads=[Beg], writes=[Beg])
                    tt("dve", sgg[h], PS[bk], eg, ALU.mult, [PSB[bk], Beg], [Bsgg[h]])
                for j in range(4):
                    bk = proj_tm(O_GK, 256, j)
                    cp("dve", ktm[:, j, :], PS[bk][:, 0:256], [PSB[bk]], [Bktm])
                    bk = proj_tm(O_GV, 512, j)
                    act(vtm[:, j, :], PS[bk], AF.Copy, [PSB[bk]], [Bvtm])
                    bk = psb()
                    mm(bk, PS[bk][:, 0:256], a1T[0:16, j * 128:(j + 1) * 128], wa2b[0:16, :], True, True, [Ba1, Bwa2])
                    tt("dve", t1, PS[bk][:, 0:256], pb[:, 0:256], ALU.add, [PSB[bk], Bpp], [Bt1])
                    stt(t2, t1, -1.0, t1, ALU.mult, ALU.max, [Bt1], [Bt2])
                    act(t2, t2, AF.Exp, [Bt2], [Bt2], scale=-1.0)
                    act(t2, t2, AF.Ln, [Bt2], [Bt2], bias=1.0)
                    stt(la[:, j, :], t1, 0.0, t2, ALU.min, ALU.subtract, [Bt1, Bt2], [Bla])
                    bk = psb()
                    mm(bk, PS[bk][:, 0:256], tri16, la[:, j, :], True, True, [Bcst, Bla])
                    act(ektm, PS[bk][:, 0:256], AF.Exp, [PSB[bk]], [Bektm], scale=-1.0)
                    tt("pool", kintm[:, j, :], ktm[:, j, :], ektm, ALU.mult, [Bktm, Bektm], [Bkintm])
                for h in range(4):
                    bkc = psb()
                    for j in range(4):
                        mm(bkc, PS[bkc][0:64, j * 128:(j + 1) * 128], la[:, j, h * 64:(h + 1) * 64], tri16, True, True, [Bla, Bcst])
                    act(eq[0:64, :], PS[bkc][0:64, :], AF.Exp, [PSB[bkc]], [Beq])
                    act(ekT[0:64, :], PS[bkc][0:64, :], AF.Exp, [PSB[bkc]], [BekT], scale=-1.0)
                    tt("pool", qin[0:64, :], qT[h][0:64, :], eq[0:64, :], ALU.mult, [BqT[h], Beq], [Bqin])
                    tt("pool", kin[0:64, :], kT[h][0:64, :], ekT[0:64, :], ALU.mult, [BkT[h], BekT], [Bkin])
                    bko = 6 + (h % 2)
                    for j in range(4):
                        js = slice(j * 128, (j + 1) * 128)
                        bka = psb()
                        mm(bka, PS[bka][:, 0:128], kin[0:64, js], qin[0:64, js], True, True, [Bkin, Bqin])
                        ab, Bab = atb[j % 2], Batb[j % 2]
                        tt("dve", ab, PS[bka][:, 0:128], tri, ALU.mult, [PSB[bka], Bcst], [Bab])
                        mm(bko, PS[bko][:, js], vtm[:, j, h * 128:(h + 1) * 128], ab, True, False, [Bvtm, Bab])
                        mm(bko, PS[bko][:, js], Sbf[h][0:64, :], qin[0:64, js], False, True, [BSb[h], Bqin])
                        bks = psb()
                        mm(bks, PS[bks][0:64, 0:128], kintm[:, j, h * 64:(h + 1) * 64], vtm[:, j, h * 128:(h + 1) * 128], True, True, [Bkintm, Bvtm])
                        dcol = eq[0:64, j * 128 + 127:j * 128 + 128]
                        act(cst_[0:64, :], PS[bks][0:64, 0:128], AF.Identity, [PSB[bks], Beq], [Bcs], scale=dcol)
                        stt(Sst[h][0:64, :], Sst[h][0:64, :], dcol, cst_[0:64, :], ALU.mult, ALU.add, [BS[h], Beq, Bcs], [BS[h]])
                        cp("pool", Sbf[h][0:64, :], Sst[h][0:64, :], [BS[h]], [BSb[h]])
                    rstd1(bko, TT, 1.0 / 128, sq1, Bsq1, rs2, Brs2)
                    stt(y1, PS[bko], pp[:, 48 + h:49 + h], rs2, ALU.mult, ALU.mult, [PSB[bko], Brs2, Bpp], [By1])
                    tt("pool", ystage[:, h, :], y1, sgg[h], ALU.mult, [By1, Bsgg[h]], [Bys])
                P.dma("sp", yT_d[0:512, t * TT:(t + 1) * TT].rearrange("(h p) s -> p h s", p=128), ystage, reads=[Bys])
                for (o_col, gcol, stage, Bst) in ((O_FQ, 52, qstage, Bqs), (O_FK, 53, kstage, Bks)):
                    for h in range(4):
                        bk = proj_fm(o_col + h * 128, 128)
                        rstd1(bk, TT, 1.0 / 128, sq1, Bsq1, rs2, Brs2)
                        stt(stage[:, h, :], PS[bk], pp[:, gcol:gcol + 1], rs2, ALU.mult, ALU.mult, [PSB[bk], Brs2, Bpp], [Bst])
                P.dma("sp", foxq_d[:, :, t * TT:(t + 1) * TT].rearrange("h p s -> p h s"), qstage, reads=[Bqs])
                P.dma("sp", foxk_d[:, :, t * TT:(t + 1) * TT].rearrange("h p s -> p h s"), kstage, reads=[Bks])
                for j in range(4):
                    bk = proj_tm(O_FV, 512, j)
                    if j % 2 == 0:
                        act(vstage[:, j, :], PS[bk], AF.Copy, [PSB[bk]], [Bvs])
                    else:
                        cp("dve", vstage[:, j, :], PS[bk], [PSB[bk]], [Bvs])
                P.dma("sp", foxv_d[t * TT:(t + 1) * TT, :].rearrange("(j p) n -> p j n", p=128), vstage, reads=[Bvs])
                bk = psb()
                for j in range(4):
                    for c in range(KC):
                        mm(bk, PS[bk][:, j * 4:(j + 1) * 4], hT[:, c, j * 128:(j + 1) * 128], wA[:, c, O_FF:O_FF + 4], c == 0, c == KC - 1, [BwA, BhT])
                lfo = logf[:, t * 16:(t + 1) * 16]
                P.op("dve", lambda e, bk=bk, lf1=lf1: e.tensor_tensor(out=lf1.rearrange("p (j h) -> p j h", h=4), in0=PS[bk][:, 0:16].rearrange("p (j h) -> p j h", h=4),
                                                             in1=pb[:, 256:260].unsqueeze(1).broadcast_to([128, 4, 4]), op=ALU.add),
                     reads=[PSB[bk], Bpp], writes=[Blf])
                stt(lf2, lf1, -1.0, lf1, ALU.mult, ALU.max, [Blf], [Blf])
                act(lf2, lf2, AF.Exp, [Blf], [Blf], scale=-1.0)
                act(lf2, lf2, AF.Ln, [Blf], [Blf], bias=1.0)
                stt(lfo, lf1, 0.0, lf2, ALU.min, ALU.subtract, [Blf], [Blogf])
            ring["lo"], ring["hi"], ring["i"] = 0, 8, 0
            NBH = NB * 4
            bk1 = psb()
            mm(bk1, PS[bk1][:, 0:NBH], tri, logf, True, True, [Bcst, Blogf])
            bk2 = psb()
            mm(bk2, PS[bk2][:, 0:NBH], ones_f, logf, True, True, [Bones, Blogf])
            bsum = carve(NBH); Bbs = Buf("bsum")
            incl = carve(NBH); Bincl = Buf("incl")
            cp("dve", bsum, PS[bk2][:, 0:NBH], [PSB[bk2]], [Bbs])
            onesrow = carve(NB); Bor = Buf("onesrow")
            P.op("dve", lambda e, onesrow=onesrow: e.memset(onesrow, 1.0), writes=[Bor])
            for h in range(4):
                bv = bsum.rearrange("p (b h) -> p h b", h=4)[:, h, :]
                iv = incl.rearrange("p (b h) -> p h b", h=4)[:, h, :]
                P.op("dve", lambda e, bv=bv, iv=iv, onesrow=onesrow: e.tensor_tensor_scan(out=iv, data0=onesrow, data1=bv, initial=0.0, op0=ALU.mult, op1=ALU.add),
                     reads=[Bbs, Bor], writes=[Bincl])
            tt("dve", incl, incl, bsum, ALU.subtract, [Bincl, Bbs], [Bincl])
            tt("dve", cum_all, PS[bk1][:, 0:NBH], incl, ALU.add, [PSB[bk1], Bincl], [Bcum])
            P.op("dve", lambda e: e.tensor_scalar_mul(out=negcum, in0=cum_all, scalar1=-1.0), reads=[Bcum], writes=[Bcum])
            if dbg:
                P.dma("sp", dbg_d["cum"], cum_all, reads=[Bcum])
            P.barrier()
            if dbg:
                for nm, src_ in (("foxq", foxq_d), ("foxk", foxk_d), ("foxv", foxv_d)):
                    P.dma("sp", dbg_d[nm], src_)
                P.barrier()

            off[0] = persist_mark
            kres2 = [carve(S, BF16) for _ in range(2)]; Bkres2 = [Buf("kres0"), Buf("kres1")]
            vres2 = [carve(NB * 128, BF16).rearrange("p (b d) -> p b d", d=128) for _ in range(2)]; Bvres2 = [Buf("vres0"), Buf("vres1")]
            qt2 = [carve(512, BF16) for _ in range(2)]; Bqt2 = [Buf("q0"), Buf("q1")]
            cq2 = [carve(512) for _ in range(2)]; Bcq2 = [Buf("cq0"), Buf("cq1")]
            dg = [carve(128) for _ in range(2)]; Bdg = [Buf("dg0"), Buf("dg1")]
            tb = [carve(512) for _ in range(3)]; Btb = [Buf(f"tb{i}") for i in range(3)]
            pt = [carve(512, BF16) for _ in range(3)]; Bpt = [Buf(f"pt{i}") for i in range(3)]
            rl = carve(512); Brl = Buf("rl")
            yst = [carve(512, BF16) for _ in range(2)]; Byst = [Buf("yst0"), Buf("yst1")]
            ring["lo"], ring["hi"], ring["i"] = 4, 8, 0
            scale = 128.0 ** -0.5
            it = 0

            def load_kv(h):
                P.dma("sp", kres2[h % 2], foxk_d[h], writes=[Bkres2[h % 2]])
                vsrc = foxv_d[:, h * 128:(h + 1) * 128].rearrange("(b p) d -> p b d", p=128)
                step = 16
                for b0 in range(0, NB, step):
                    b1 = min(NB, b0 + step)
                    P.dma("sp", vres2[h % 2][:, b0:b1, :], vsrc[:, b0:b1, :], writes=[Bvres2[h % 2]])

            iters = [(h, qt) for h in range(4) for qt in range(NT)]

            def load_q(i):
                h, qt = iters[i]
                P.dma("sp", qt2[i % 2], foxq_d[h][:, qt * TT:(qt + 1) * TT], writes=[Bqt2[i % 2]])

            load_kv(0)
            load_q(0)
            for i, (h, qt) in enumerate(iters):
                par = i % 2
                kres, Bkres, vres, Bvres = kres2[h % 2], Bkres2[h % 2], vres2[h % 2], Bvres2[h % 2]
                if qt == 0 and h + 1 < 4:
                    load_kv(h + 1)
                if i + 1 < len(iters):
                    load_q(i + 1)
                qtile, Bq = qt2[par], Bqt2[par]
                cq, Bcq = cq2[par], Bcq2[par]
                bkq = psb()
                for j in range(4):
                    blk = qt * 4 + j
                    d_, Bd_ = dg[j % 2], Bdg[j % 2]
                    P.op("dve", lambda e, d_=d_, blk=blk, h=h: e.tensor_scalar_mul(out=d_, in0=ident, scalar1=cum_all[:, blk * 4 + h:blk * 4 + h + 1]),
                         reads=[Bcst, Bcum], writes=[Bd_])
                    mm(bkq, PS[bkq][:, j * 128:(j + 1) * 128], ones_f, d_, True, True, [Bones, Bd_])
                act(cq, PS[bkq], AF.Copy, [PSB[bkq]], [Bcq])
                bo, bl = (0, 1) if par == 0 else (2, 3)
                nkb = qt * 4 + 4
                for kb in range(nkb):
                    j = kb - qt * 4
                    c0 = max(0, j) * 128
                    cs_ = slice(c0, TT)
                    bks = psb()
                    mm(bks, PS[bks][:, cs_], kres[:, kb * 128:(kb + 1) * 128], qtile[:, cs_], True, True, [Bkres, Bq])
                    tbuf, Btbuf = tb[it % 3], Btb[it % 3]
                    pbuf, Bpbuf = pt[it % 3], Bpt[it % 3]
                    it += 1
                    stt(tbuf[:, cs_], PS[bks][:, cs_], scale, cq[:, cs_], ALU.mult, ALU.add, [PSB[bks], Bcq], [Btbuf])
                    if j >= 0:
                        tt("pool", tbuf[:, c0:c0 + 128], tbuf[:, c0:c0 + 128], maskneg, ALU.add, [Btbuf, Bcst], [Btbuf])
                    act(pbuf[:, cs_], tbuf[:, cs_], AF.Exp, [Btbuf, Bcum], [Bpbuf], bias=negcum[:, kb * 4 + h:kb * 4 + h + 1])
                    if dbg and i == 0 and kb == 0:
                        P.dma("sp", dbg_d["cq"], cq, reads=[Bcq])
                        P.dma("sp", dbg_d["tb"], tbuf, reads=[Btbuf])
                        P.dma("sp", dbg_d["pt"], pbuf, reads=[Bpbuf])
                    mm(bo, PS[bo][:, cs_], vres[:, kb, :], pbuf[:, cs_], kb == 0, kb == nkb - 1, [Bvres, Bpbuf])
                    mm(bl, PS[bl][:, cs_], ones_b, pbuf[:, cs_], kb == 0, kb == nkb - 1, [Bones, Bpbuf])
                act(rl, PS[bl], AF.Copy, [PSB[bl]], [Brl])
                P.op("dve", lambda e, rl=rl: e.reciprocal(out=rl, in_=rl), reads=[Brl], writes=[Brl])
                ys, Bys_ = yst[par], Byst[par]
                if dbg and i == 0:
                    P.dma("sp", dbg_d["rl"], rl, reads=[Brl])
                    act(tb[0], PS[bl], AF.Copy, [PSB[bl]], [Btb[0]])
                    P.dma("sp", dbg_d["lraw"], tb[0], reads=[Btb[0]])
                    act(tb[1], PS[bo], AF.Copy, [PSB[bo]], [Btb[1]])
                    P.dma("sp", dbg_d["oraw"], tb[1], reads=[Btb[1]])
                tt("dve", ys, PS[bo], rl, ALU.mult, [PSB[bo], Brl], [Bys_])
                P.dma("sp", yT_d[512 + h * 128:512 + (h + 1) * 128, qt * TT:(qt + 1) * TT], ys, reads=[Bys_])
            ring["lo"], ring["hi"], ring["i"] = 0, 8, 0
            P.barrier()
            if dbg:
                P.dma("sp", dbg_d["yT"], yT_d)
                P.barrier()

            off[0] = persist_mark
            wC = carve(KC * WC_COLS, BF16).rearrange("p (c n) -> p c n", c=KC); BwC = Buf("wC")
            wB = carve(12 * D, BF16).rearrange("p (c n) -> p c n", c=12); BwB = Buf("wB")
            wO = carve(KC * D, BF16).rearrange("p (c n) -> p c n", c=KC); BwO = Buf("wO")
            wait_weight(l, "in"); wait_weight(l, "br"); wait_weight(l, "out")
            for c in range(KC):
                P.dma("sp", wC[:, c, :], b_in[l][c * 128:(c + 1) * 128, WA_COLS:INW], writes=[BwC])
            for c in range(12):
                P.dma("sp", wB[:, c, :], b_br[l][c * 128:(c + 1) * 128, :], writes=[BwB])
            for c in range(KC):
                P.dma("sp", wO[:, c, :], b_out[l][c * 128:(c + 1) * 128, :], writes=[BwO])
            xt = carve(KC * TT).rearrange("p (c n) -> p c n", c=KC); Bxt = Buf("xt")
            sqr = [carve(512, BF16) for _ in range(2)]; Bsqr = [Buf("sqr0"), Buf("sqr1")]
            hT = carve(KC * TT, BF16).rearrange("p (c n) -> p c n", c=KC); BhT = Buf("hT")
            rs = carve(512); Brs = Buf("rs")
            rs2 = carve(512); Brs2 = Buf("rs2")
            sq1 = carve(512, BF16); Bsq1 = Buf("sq1")
            yt = carve(12 * TT, BF16).rearrange("p (c n) -> p c n", c=12)
            Byt = Buf("yt"); Bym = Buf("ym")
            mqn = carve(512, BF16); Bmqn = Buf("mqn")
            ptm = [carve(512, BF16) for _ in range(2)]; Bptm = [Buf("ptm0"), Buf("ptm1")]
            rl = carve(512); Brl = Buf("rl")
            gate = [carve(512) for _ in range(3)]; Bgate = [Buf(f"gate{i}") for i in range(3)]
            term = [carve(512) for _ in range(3)]; Bterm = [Buf(f"term{i}") for i in range(3)]
            mg = carve(KC * TT, BF16).rearrange("p (c n) -> p c n", c=KC); Bmg = [Buf(f"mg{c}") for c in range(KC)]
            yTv = yT_d.rearrange("(c p) s -> p c s", p=128)

            for t in range(NT):
                P.dma("sp", xt, xTv[:, :, t * TT:(t + 1) * TT], writes=[Bxt])
                P.dma("sp", yt[:, 0:8, :], yTv[:, :, t * TT:(t + 1) * TT], writes=[Byt])
                norm_tile(xt, Bxt, 0, TT, hT, BhT, sqr, Bsqr, rs, Brs)
                for h in range(4):
                    bk = psb()
                    for c in range(KC):
                        mm(bk, PS[bk], wC[:, c, h * 128:(h + 1) * 128], hT[:, c, :], c == 0, c == KC - 1, [BwC, BhT])
                    rstd1(bk, TT, 1.0 / 128, sq1, Bsq1, rs2, Brs2)
                    stt(mqn, PS[bk], pp[:, 54:55], rs2, ALU.mult, ALU.mult, [PSB[bk], Brs2, Bpp], [Bmqn])
                    bo = psb(); bl = psb()
                    for mb in range(2):
                        bks = psb()
                        mm(bks, PS[bks], mkn[:, h, mb * 128:(mb + 1) * 128], mqn, True, True, [Bmkn, Bmqn])
                        act(ptm[mb], PS[bks], AF.Exp, [PSB[bks]], [Bptm[mb]], scale=128.0 ** -0.5)
                        mm(bo, PS[bo], mv[:, mb, h * 128:(h + 1) * 128], ptm[mb], mb == 0, mb == 1, [Bmv, Bptm[mb]])
                        mm(bl, PS[bl], ones_b, ptm[mb], mb == 0, mb == 1, [Bones, Bptm[mb]])
                    act(rl, PS[bl], AF.Copy, [PSB[bl]], [Brl])
                    P.op("dve", lambda e, rl=rl: e.reciprocal(out=rl, in_=rl), reads=[Brl], writes=[Brl])
                    tt("dve", yt[:, 8 + h, :], PS[bo], rl, ALU.mult, [PSB[bo], Brl], [Bym])
                for cb in range(KC):
                    for i in range(3):
                        bkb = psb()
                        for kc in range(4):
                            mm(bkb, PS[bkb], wB[:, i * 4 + kc, cb * 128:(cb + 1) * 128], yt[:, i * 4 + kc, :], kc == 0, kc == 3, [BwB, Byt, Bym])
                        bkg = psb()
                        gc0 = 512 + i * D + cb * 128
                        for c in range(KC):
                            mm(bkg, PS[bkg], wC[:, c, gc0:gc0 + 128], hT[:, c, :], c == 0, c == KC - 1, [BwC, BhT])
                        act(gate[i], PS[bkg], AF.Sigmoid, [PSB[bkg], Bpp], [Bgate[i]], bias=pp[:, 24 + i * 8 + cb:25 + i * 8 + cb])
                        tt("dve", term[i], PS[bkb], gate[i], ALU.mult, [PSB[bkb], Bgate[i]], [Bterm[i]])
                    tt("pool", term[0], term[0], term[1], ALU.add, [Bterm[0], Bterm[1]], [Bterm[0]])
                    tt("pool", mg[:, cb, :], term[0], term[2], ALU.add, [Bterm[0], Bterm[2]], [Bmg[cb]])
                for cb in range(KC):
                    bk = psb()
                    for kc in range(KC):
                        mm(bk, PS[bk], wO[:, kc, cb * 128:(cb + 1) * 128], mg[:, kc, :], kc == 0, kc == KC - 1, [BwO] + Bmg)
                    tt("dve", xt[:, cb, :], xt[:, cb, :], PS[bk], ALU.add, [Bxt, PSB[bk]], [Bxt])
                P.dma("sp", xTv[:, :, t * TT:(t + 1) * TT], xt, reads=[Bxt])
                if dbg:
                    P.dma("sp", dbg_d["xmid"].rearrange("(c p) s -> p c s", p=128)[:, :, t * TT:(t + 1) * TT], xt, reads=[Bxt])
            P.barrier()

            off[0] = persist_mark
            wG = carve(KC * DFF, BF16).rearrange("p (c n) -> p c n", c=KC); BwG = Buf("wG")
            wU = carve(KC * DFF, BF16).rearrange("p (c n) -> p c n", c=KC); BwU = Buf("wU")
            wait_weight(l, "fg"); wait_weight(l, "fu"); wait_weight(l, "fd")
            for c in range(KC):
                P.dma("sp", wG[:, c, :], b_fg[l][c * 128:(c + 1) * 128, :], writes=[BwG])
                P.dma("sp", wU[:, c, :], b_fu[l][c * 128:(c + 1) * 128, :], writes=[BwU])
            wD2 = [carve(NF * 128, BF16).rearrange("p (f n) -> p f n", f=NF) for _ in range(2)]
            BwD2 = [Buf("wD0"), Buf("wD1")]
            xt = carve(KC * TT).rearrange("p (c n) -> p c n", c=KC); Bxt = Buf("xt")
            sqr = [carve(512, BF16) for _ in range(2)]; Bsqr = [Buf("sqr0"), Buf("sqr1")]
            hT = carve(KC * TT, BF16).rearrange("p (c n) -> p c n", c=KC); BhT = Buf("hT")
            rs = carve(512); Brs = Buf("rs")
            aT = carve(NF * TT, BF16).rearrange("p (f n) -> p f n", f=NF); BaT = [Buf(f"aT{f}") for f in range(NF)]
            sg = [carve(512) for _ in range(2)]; Bsg = [Buf("sg0"), Buf("sg1")]
            ost = carve(4 * D).rearrange("p (j n) -> p j n", j=4) if last_layer else None
            Bost = Buf("ost")
            b_fdv = b_fd[l].rearrange("(f p) n -> p f n", p=128)

            wdi = 0
            for t in range(NT):
                P.dma("sp", xt, xTv[:, :, t * TT:(t + 1) * TT], writes=[Bxt])
                norm_tile(xt, Bxt, 8, TT, hT, BhT, sqr, Bsqr, rs, Brs)
                for f in range(NF):
                    bg = psb()
                    for c in range(KC):
                        mm(bg, PS[bg], wG[:, c, f * 128:(f + 1) * 128], hT[:, c, :], c == 0, c == KC - 1, [BwG, BhT])
                    bu = psb()
                    for c in range(KC):
                        mm(bu, PS[bu], wU[:, c, f * 128:(f + 1) * 128], hT[:, c, :], c == 0, c == KC - 1, [BwU, BhT])
                    act(sg[f % 2], PS[bg], AF.Silu, [PSB[bg]], [Bsg[f % 2]])
                    tt("dve", aT[:, f, :], PS[bu], sg[f % 2], ALU.mult, [PSB[bu], Bsg[f % 2]], [BaT[f]])
                for cb in range(KC):
                    wD, BwD = wD2[wdi % 2], BwD2[wdi % 2]
                    wdi += 1
                    P.dma("sp", wD, b_fdv[:, :, cb * 128:(cb + 1) * 128], writes=[BwD])
                    bk = psb()
                    for f in range(NF):
                        mm(bk, PS[bk], wD[:, f, :], aT[:, f, :], f == 0, f == NF - 1, [BwD] + BaT)
                    tt("dve", xt[:, cb, :], xt[:, cb, :], PS[bk], ALU.add, [Bxt, PSB[bk]], [Bxt])
                if not last_layer:
                    P.dma("sp", xTv[:, :, t * TT:(t + 1) * TT], xt, reads=[Bxt])
                else:
                    for j in range(4):
                        for half in range(2):
                            bk = psb()
                            for q in range(4):
                                cb = half * 4 + q
                                P.op("pe", lambda e, bk=bk, q=q, cb=cb, j=j, xt=xt: e.transpose(PS[bk][:, q * 128:(q + 1) * 128], xt[:, cb, j * 128:(j + 1) * 128], ident),
                                     reads=[Bxt, Bcst], writes=[PSB[bk]])
                            if half == 0:
                                act(ost[:, j, 0:512], PS[bk], AF.Copy, [PSB[bk]], [Bost])
                            else:
                                cp("dve", ost[:, j, 512:1024], PS[bk], [PSB[bk]], [Bost])
                    P.dma("sp", out_d[t * TT:(t + 1) * TT, :].rearrange("(j p) n -> p j n", p=128), ost, reads=[Bost])
            P.barrier()
        P.finalize()
    return nc


def host_consts():
    ident = np.eye(128, dtype=np.float32)
    s = np.arange(128)[:, None]
    c = np.arange(128)[None, :]
    tri = (s <= c).astype(np.float32)
    return np.ascontiguousarray(np.concatenate([ident, tri, tri / 16.0, (tri - 1.0) * (-NEG)], axis=1))


def host_params(inp, depth):
    pp = np.zeros((depth, 128, NP_), np.float32)
    pb = np.zeros((depth, 128, 260), np.float32)
    for l in range(depth):
        pp[l, :, 0:8] = inp["g_mix"][l].reshape(8, 128).T
        pp[l, :, 8:16] = inp["g_ffn"][l].reshape(8, 128).T
        pp[l, :, 16:24] = inp["g_mem"][l].reshape(8, 128).T
        pp[l, :, 24:48] = inp["b_gate"][l].reshape(24, 128).T
        pp[l, :, 48:52] = inp["g_gla_out"][l].reshape(4, 128).T
        pp[l, :, 52] = inp["g_fox_q"][l]
        pp[l, :, 53] = inp["g_fox_k"][l]
        pp[l, :, 54] = inp["g_mem_q"][l]
        pp[l, :, 55] = inp["g_mem_k"][l]
        pb[l, :, 0:256] = inp["b_gla_a"][l][None, :]
        pb[l, :, 256:260] = inp["b_fox_f"][l][None, :]
    return pp, pb


def make_in_maps(inp, S, depth, ncores):
    inp = {k: np.asarray(v) for k, v in inp.items()}
    pp, pb = host_params(inp, depth)
    consts = host_consts()
    B = inp["x"].shape[0]
    shared = dict(
        w_in=np.ascontiguousarray(inp["w_in"][:depth]), w_mem_kv=np.ascontiguousarray(inp["w_mem_kv"][:depth]),
        w_branch=np.ascontiguousarray(inp["w_branch"][:depth].reshape(depth, 1536, D)),
        w_out=np.ascontiguousarray(inp["w_out"][:depth]), w_ffn_gate=np.ascontiguousarray(inp["w_ffn_gate"][:depth]),
        w_ffn_up=np.ascontiguousarray(inp["w_ffn_up"][:depth]), w_ffn_down=np.ascontiguousarray(inp["w_ffn_down"][:depth]),
        w_gla_a2=np.ascontiguousarray(inp["w_gla_a2"][:depth]), pp=pp, pb=pb, consts=consts)
    maps = []
    for cid in range(ncores):
        b = cid % B
        m = dict(shared)
        m["x"] = np.ascontiguousarray(inp["x"][b, :S])
        m["mem"] = np.ascontiguousarray(inp["mem"][b])
        maps.append(m)
    return maps


_NC_CACHE = {}


def kernel(**inputs):
    S, depth = 8192, 4
    key = (S, depth)
    if key not in _NC_CACHE:
        _NC_CACHE[key] = build_program(S, depth)
    nc = _NC_CACHE[key]
    maps = make_in_maps(inputs, S, depth, 8)
    res = run_bass_kernel_spmd(nc, maps, core_ids=list(range(8)))
    B = np.asarray(inputs["x"]).shape[0]
    out = np.stack([np.asarray(res.results[b]["out"]) for b in range(B)], axis=0)
    return out.astype(np.float32)
```

```python
import numpy as np
import concourse.bass as bass
import concourse.mybir as mybir
from concourse.bass_utils import run_bass_kernel_spmd
from contextlib import ExitStack

F32 = mybir.dt.float32
BF16 = mybir.dt.bfloat16
AF = mybir.ActivationFunctionType
ALU = mybir.AluOpType

ENGS = ("pe", "act", "dve", "pool", "sp")
EPOCH = 30000
N_DMA_SEMS = 64

D = 1024
KC = 8
TT = 512
DFF = 2816
NF = DFF // 128
INW = 6676
WA_COLS = 3092
WC_COLS = INW - WA_COLS
O_GQ, O_GK, O_GV, O_GG, O_GA, O_FQ, O_FK, O_FV, O_FF = 0, 256, 512, 1024, 1536, 1552, 2064, 2576, 3088
MEM = 256
EPS = 1e-6
NP_ = 56
NEG = -30000.0


class Buf:
    __slots__ = ("name", "last_w", "readers", "const", "psum")
    registry = []

    def __init__(self, name, psum=False):
        self.name = name
        self.last_w = None
        self.readers = []
        self.const = False
        self.psum = psum
        Buf.registry.append(self)


class Op:
    __slots__ = ("eng", "emit", "deps", "needs_inc", "seq", "is_dma", "dsem", "dval", "idx", "extra_waits")


class Prog:
    def __init__(self, nc, es):
        self.nc = nc
        self.es = es
        self.ops = {e: [] for e in ENGS}
        self.dma_sems = [es.enter_context(nc.semaphore(f"dq{i}")) for i in range(N_DMA_SEMS)]
        self.dma_cnt = [0] * N_DMA_SEMS
        self.dma_last = [None] * N_DMA_SEMS
        self.dma_i = 0
        self.eng_sems = {}
        self.pending_dma = []

    def _new(self, eng):
        o = Op()
        o.eng = eng
        o.emit = None
        o.needs_inc = False
        o.seq = None
        o.is_dma = False
        o.dsem = None
        o.dval = 0
        o.extra_waits = []
        o.deps = []
        return o

    def _record(self, o, reads, writes):
        deps = set()
        for b in reads:
            if b.last_w is not None:
                deps.add(b.last_w)
            if b.psum:
                for r_ in b.readers:
                    if r_.eng != o.eng:
                        deps.add(r_)
        for b in writes:
            if b.last_w is not None:
                deps.add(b.last_w)
            deps.update(b.readers)
        lst = self.ops[o.eng]
        o.idx = len(lst)
        fdeps = []
        for d in deps:
            if d is o:
                continue
            if not d.is_dma and not o.is_dma and d.eng == o.eng:
                if o.eng == "pe":
                    continue
                if o.idx - d.idx > 2:
                    continue
            fdeps.append(d)
        o.deps = fdeps
        for d in fdeps:
            d.needs_inc = True
        for b in writes:
            b.last_w = o
            b.readers = []
        for b in reads:
            if b.const:
                continue
            if all(b is not w for w in writes):
                b.readers.append(o)
        lst.append(o)
        return o

    def op(self, eng, emit, reads=(), writes=()):
        o = self._new(eng)
        o.emit = emit
        return self._record(o, reads, writes)

    def dma(self, queue, out, in_, reads=(), writes=()):
        o = self._new(queue)
        o.is_dma = True
        i = self.dma_i % N_DMA_SEMS
        self.dma_i += 1
        if self.dma_last[i] is not None:
            o.extra_waits.append(self.dma_last[i])
        self.dma_cnt[i] += 16
        o.dsem = self.dma_sems[i]
        o.dval = self.dma_cnt[i]
        self.dma_last[i] = o
        o.emit = lambda e, out=out, in_=in_: e.dma_start(out=out, in_=in_)
        self.pending_dma.append(o)
        return self._record(o, reads, writes)

    def barrier(self, engines=ENGS):
        lasts = []
        for e in ENGS:
            for o in reversed(self.ops[e]):
                if not o.is_dma and o.emit is not None:
                    lasts.append(o)
                    break
        latest = {}
        for d in self.pending_dma:
            latest[d.dsem.num] = d
        dm = list(latest.values())
        self.pending_dma = []
        for e in engines:
            o = self._new(e)
            o.idx = len(self.ops[e])
            o.deps = list(lasts) + dm
            for d in o.deps:
                d.needs_inc = True
            self.ops[e].append(o)
        for b in Buf.registry:
            b.last_w = None
            b.readers = []

    def finalize(self):
        nc = self.nc
        for e in ENGS:
            n = 0
            for o in self.ops[e]:
                if o.needs_inc and not o.is_dma and o.emit is not None:
                    n += 1
                    o.seq = n
            nep = (n + EPOCH - 1) // EPOCH
            self.eng_sems[e] = [self.es.enter_context(nc.semaphore(f"es_{e}{k}")) for k in range(max(nep, 1))]

        def handle(d):
            if d.is_dma:
                return d.dsem, d.dval
            k = (d.seq - 1) // EPOCH
            return self.eng_sems[d.eng][k], d.seq - k * EPOCH

        engobj = {"pe": "tensor", "act": "scalar", "dve": "vector", "pool": "gpsimd", "sp": "sync"}
        with nc.Block() as block:
            def make(ename):
                def body(eng):
                    waited = {}
                    for o in self.ops[ename]:
                        for d in list(o.deps) + list(o.extra_waits):
                            s, v = handle(d)
                            if waited.get(s.num, 0) >= v:
                                continue
                            waited[s.num] = v
                            eng.wait_ge(s, v)
                        if o.emit is None:
                            continue
                        ins = o.emit(eng)
                        if o.is_dma:
                            ins.then_inc(o.dsem, 16)
                        elif o.needs_inc:
                            s, v = handle(o)
                            ins.then_inc(s, 1)
                return body
            for ename in ENGS:
                if self.ops[ename]:
                    getattr(block, engobj[ename])(make(ename))


def build_program(S, depth, dbg=False):
    NT = S // TT
    NB = S // 128
    nc = bass.Bass("TRN2", target_bir_lowering=False)
    Buf.registry = []

    def din(name, shape, dt=F32):
        return nc.dram_tensor(name, list(shape), dt, kind="ExternalInput").ap()

    def dint(name, shape, dt):
        return nc.dram_tensor(name, list(shape), dt, kind="Internal").ap()

    x_in = din("x", [S, D])
    mem_in = din("mem", [MEM, D])
    w_in = din("w_in", [depth, D, INW])
    w_mkv = din("w_mem_kv", [depth, D, 2 * 512])
    w_br = din("w_branch", [depth, 3 * 512, D])
    w_out = din("w_out", [depth, D, D])
    w_fg = din("w_ffn_gate", [depth, D, DFF])
    w_fu = din("w_ffn_up", [depth, D, DFF])
    w_fd = din("w_ffn_down", [depth, DFF, D])
    w_a2 = din("w_gla_a2", [depth, 16, 256])
    pp_in = din("pp", [depth, 128, NP_])
    pb_in = din("pb", [depth, 128, 260])
    consts_in = din("consts", [128, 4 * 128])
    out_d = nc.dram_tensor("out", [S, D], F32, kind="ExternalOutput").ap()

    b_in = dint("b_in", [depth, D, INW], BF16)
    b_mkv = dint("b_mkv", [depth, D, 1024], BF16)
    b_br = dint("b_br", [depth, 1536, D], BF16)
    b_out = dint("b_out", [depth, D, D], BF16)
    b_fg = dint("b_fg", [depth, D, DFF], BF16)
    b_fu = dint("b_fu", [depth, D, DFF], BF16)
    b_fd = dint("b_fd", [depth, DFF, D], BF16)
    xT_d = dint("xT", [D, S], F32)
    foxq_d = dint("foxq", [4, 128, S], BF16)
    foxk_d = dint("foxk", [4, 128, S], BF16)
    foxv_d = dint("foxv", [S, 512], BF16)
    yT_d = dint("yT", [D, S], BF16)
    dbg_d = {}
    if dbg:
        dbg_d["xT0"] = nc.dram_tensor("dbg_xT0", [D, S], F32, kind="ExternalOutput").ap()
        dbg_d["yT"] = nc.dram_tensor("dbg_yT", [D, S], BF16, kind="ExternalOutput").ap()
        dbg_d["foxq"] = nc.dram_tensor("dbg_foxq", [4, 128, S], BF16, kind="ExternalOutput").ap()
        dbg_d["foxk"] = nc.dram_tensor("dbg_foxk", [4, 128, S], BF16, kind="ExternalOutput").ap()
        dbg_d["foxv"] = nc.dram_tensor("dbg_foxv", [S, 512], BF16, kind="ExternalOutput").ap()
        dbg_d["cum"] = nc.dram_tensor("dbg_cum", [128, NB * 4], F32, kind="ExternalOutput").ap()
        dbg_d["xmid"] = nc.dram_tensor("dbg_xmid", [D, S], F32, kind="ExternalOutput").ap()
        dbg_d["cq"] = nc.dram_tensor("dbg_cq", [128, 512], F32, kind="ExternalOutput").ap()
        dbg_d["tb"] = nc.dram_tensor("dbg_tb", [128, 512], F32, kind="ExternalOutput").ap()
        dbg_d["pt"] = nc.dram_tensor("dbg_pt", [128, 512], BF16, kind="ExternalOutput").ap()
        dbg_d["rl"] = nc.dram_tensor("dbg_rl", [128, 512], F32, kind="ExternalOutput").ap()
        dbg_d["lraw"] = nc.dram_tensor("dbg_lraw", [128, 512], F32, kind="ExternalOutput").ap()
        dbg_d["oraw"] = nc.dram_tensor("dbg_oraw", [128, 512], F32, kind="ExternalOutput").ap()

    with ExitStack() as es:
        P = Prog(nc, es)
        ARENA_F32 = 52736
        arena_t = es.enter_context(nc.sbuf_tensor("arena", [128, ARENA_F32], F32))
        psum_t = es.enter_context(nc.psum_tensor("psum", [128, 8, 512], F32))
        A = arena_t[:, :]
        PSA = psum_t[:, :, :]
        PS = [PSA[:, i, :] for i in range(8)]
        PSB = [Buf(f"ps{i}", psum=True) for i in range(8)]
        off = [0]

        def carve(nelem, dt=F32):
            n32 = nelem if dt == F32 else (nelem + 1) // 2
            assert off[0] + n32 <= ARENA_F32, f"SBUF arena overflow {off[0] + n32}"
            a = A[:, off[0]:off[0] + n32]
            off[0] += n32
            if dt != F32:
                a = a.bitcast(dt)
            return a

        ring = {"i": 0, "lo": 0, "hi": 8}

        def psb():
            n = ring["hi"] - ring["lo"]
            i = ring["lo"] + ring["i"] % n
            ring["i"] += 1
            return i

        cst = carve(512)
        ident = cst[:, 0:128]
        tri = cst[:, 128:256]
        tri16 = cst[:, 256:384]
        maskneg = cst[:, 384:512]
        Bcst = Buf("cst")
        ones_f = carve(128)
        ones_b = carve(128, BF16)
        Bones = Buf("ones")
        pp = carve(NP_)
        pb = carve(260)
        Bpp = Buf("pp")
        wa2f = carve(256)
        wa2b = carve(256, BF16)
        Bwa2 = Buf("wa2")
        memT = carve(KC * MEM).rearrange("p (c n) -> p c n", c=KC)
        BmemT = Buf("memT")
        logf = carve(NB * 4)
        Blogf = Buf("logf")
        cum_all = carve(NB * 4)
        negcum = carve(NB * 4)
        Bcum = Buf("cum")
        Sst = [carve(128) for _ in range(4)]
        Sbf = [carve(128, BF16) for _ in range(4)]
        BS = [Buf(f"S{h}") for h in range(4)]
        BSb = [Buf(f"Sb{h}") for h in range(4)]
        mkn = carve(4 * MEM, BF16).rearrange("p (h n) -> p h n", h=4)
        mv = carve(2 * 512, BF16).rearrange("p (b n) -> p b n", b=2)
        Bmkn = Buf("mkn")
        Bmv = Buf("mv")
        persist_mark = off[0]

        P.dma("sp", cst, consts_in, writes=[Bcst])
        P.op("dve", lambda e: e.memset(ones_f, 1.0), writes=[Bones])
        P.op("dve", lambda e: e.memset(ones_b, 1.0), writes=[Bones])

        pieces = {}
        for l in range(depth):
            for name, src_, dst_, ne in (("in", w_in[l], b_in[l], D * INW), ("mkv", w_mkv[l], b_mkv[l], D * 1024),
                                         ("br", w_br[l], b_br[l], 1536 * D), ("out", w_out[l], b_out[l], D * D),
                                         ("fg", w_fg[l], b_fg[l], D * DFF), ("fu", w_fu[l], b_fu[l], D * DFF),
                                         ("fd", w_fd[l], b_fd[l], DFF * D)):
                rows = ne // 1024
                s2 = src_.rearrange("a b -> (a b)").rearrange("(r c) -> r c", c=1024)
                d2 = dst_.rearrange("a b -> (a b)").rearrange("(r c) -> r c", c=1024)
                lst = []
                r = 0
                while r < rows:
                    n = min(2048, rows - r)
                    lst.append(P.dma("pool", d2[r:r + n, :], s2[r:r + n, :]))
                    r += n
                pieces[(l, name)] = lst
        P.pending_dma = [o for o in P.pending_dma if o.eng != "pool"]

        def wait_weight(l, name, queue="sp"):
            o = P._new(queue)
            o.idx = len(P.ops[queue])
            o.deps = list(pieces[(l, name)])
            P.ops[queue].append(o)

        def mm(bank, out_ap, lhsT, rhs, start, stop, reads):
            P.op("pe", lambda e: e.matmul(out_ap, lhsT=lhsT, rhs=rhs, start=start, stop=stop),
                 reads=reads, writes=[PSB[bank]])

        def act(out, in_, func, reads, writes, bias=0.0, scale=1.0):
            P.op("act", lambda e: e.activation(out=out, in_=in_, func=func, bias=bias, scale=scale),
                 reads=reads, writes=writes)

        def stt(out, in0, scalar, in1, op0, op1, reads, writes, eng="dve"):
            P.op(eng, lambda e: e.scalar_tensor_tensor(out=out, in0=in0, scalar=scalar, in1=in1, op0=op0, op1=op1),
                 reads=reads, writes=writes)

        def tt(eng, out, in0, in1, op, reads, writes):
            P.op(eng, lambda e: e.tensor_tensor(out=out, in0=in0, in1=in1, op=op), reads=reads, writes=writes)

        def cp(eng, out, in_, reads, writes):
            P.op(eng, lambda e: e.tensor_copy(out=out, in_=in_), reads=reads, writes=writes)

        def rstd1(bank_in, n, inv_dim, sq1, Bsq1, rstd, Brstd):
            act(sq1[:, 0:n], PS[bank_in][:, 0:n], AF.Square, [PSB[bank_in]], [Bsq1])
            bk = psb()
            mm(bk, PS[bk][:, 0:n], ones_b, sq1[:, 0:n], True, True, [Bones, Bsq1])
            act(rstd[:, 0:n], PS[bk][:, 0:n], AF.Ln, [PSB[bk]], [Brstd], bias=EPS, scale=inv_dim)
            act(rstd[:, 0:n], rstd[:, 0:n], AF.Exp, [Brstd], [Brstd], scale=-0.5)

        def norm_tile(xt, Bxt, gcol0, n, hT, BhT, sqr, Bsqr, rstd, Brstd):
            bk = psb()
            for c in range(KC):
                s_, bs_ = sqr[c % 2], Bsqr[c % 2]
                act(s_[:, 0:n], xt[:, c, :], AF.Square, [Bxt], [bs_])
                mm(bk, PS[bk][:, 0:n], ones_b, s_[:, 0:n], c == 0, c == KC - 1, [Bones, bs_])
            act(rstd[:, 0:n], PS[bk][:, 0:n], AF.Ln, [PSB[bk]], [Brstd], bias=EPS, scale=1.0 / D)
            act(rstd[:, 0:n], rstd[:, 0:n], AF.Exp, [Brstd], [Brstd], scale=-0.5)
            for c in range(KC):
                stt(hT[:, c, :], xt[:, c, :], pp[:, gcol0 + c:gcol0 + c + 1], rstd[:, 0:n], ALU.mult, ALU.mult,
                    [Bxt, Brstd, Bpp], [BhT])

        xs2 = [carve(4 * D).rearrange("p (j n) -> p j n", j=4) for _ in range(2)]
        Bxs2 = [Buf("xs0"), Buf("xs1")]
        xo2 = [carve(KC * TT).rearrange("p (c n) -> p c n", c=KC) for _ in range(2)]
        Bxo2 = [Buf("xo0"), Buf("xo1")]
        xTv = xT_d.rearrange("(c p) s -> p c s", p=128)
        for t in range(NT):
            xs, Bxs, xo, Bxo = xs2[t % 2], Bxs2[t % 2], xo2[t % 2], Bxo2[t % 2]
            P.dma("sp", xs, x_in[t * TT:(t + 1) * TT, :].rearrange("(j p) n -> p j n", p=128), writes=[Bxs])
            for c in range(KC):
                bk = psb()
                for j in range(4):
                    P.op("pe", lambda e, bk=bk, j=j, c=c, xs=xs: e.transpose(PS[bk][:, j * 128:(j + 1) * 128], xs[:, j, c * 128:(c + 1) * 128], ident),
                         reads=[Bxs, Bcst], writes=[PSB[bk]])
                if c % 2 == 0:
                    cp("dve", xo[:, c, :], PS[bk], [PSB[bk]], [Bxo])
                else:
                    act(xo[:, c, :], PS[bk], AF.Copy, [PSB[bk]], [Bxo])
            P.dma("sp", xTv[:, :, t * TT:(t + 1) * TT], xo, reads=[Bxo])
            if dbg:
                P.dma("sp", dbg_d["xT0"].rearrange("(c p) s -> p c s", p=128)[:, :, t * TT:(t + 1) * TT], xo, reads=[Bxo])
        xs = xs2[0]
        P.dma("sp", xs[:, 0:2, :], mem_in.rearrange("(j p) n -> p j n", p=128), writes=[Bxs2[0]])
        for c in range(KC):
            bk = psb()
            for j in range(2):
                P.op("pe", lambda e, bk=bk, j=j, c=c: e.transpose(PS[bk][:, j * 128:(j + 1) * 128], xs[:, j, c * 128:(c + 1) * 128], ident),
                     reads=[Bxs2[0], Bcst], writes=[PSB[bk]])
            cp("dve", memT[:, c, :], PS[bk][:, 0:256], [PSB[bk]], [BmemT])
        P.barrier()

        for l in range(depth):
            last_layer = (l == depth - 1)
            P.dma("sp", pp, pp_in[l], writes=[Bpp])
            P.dma("sp", pb, pb_in[l], writes=[Bpp])
            P.dma("sp", wa2f[0:16, :], w_a2[l], writes=[Bwa2])
            cp("dve", wa2b[0:16, :], wa2f[0:16, :], [Bwa2], [Bwa2])
            for h in range(4):
                P.op("dve", lambda e, h=h: e.memset(Sst[h][0:64, :], 0.0), writes=[BS[h]])
                P.op("dve", lambda e, h=h: e.memset(Sbf[h][0:64, :], 0.0), writes=[BSb[h]])

            off[0] = persist_mark
            wm = carve(KC * 1024, BF16).rearrange("p (c n) -> p c n", c=KC)
            Bwm = Buf("wm")
            wait_weight(l, "mkv")
            for c in range(KC):
                P.dma("sp", wm[:, c, :], b_mkv[l][c * 128:(c + 1) * 128, :], writes=[Bwm])
            sqr = [carve(512, BF16) for _ in range(2)]; Bsqr = [Buf("sqr0"), Buf("sqr1")]
            mn = carve(KC * MEM, BF16).rearrange("p (c n) -> p c n", c=KC); Bmn = Buf("mn")
            rs = carve(512); Brs = Buf("rs")
            sq1 = carve(512, BF16); Bsq1 = Buf("sq1")
            norm_tile(memT, BmemT, 16, MEM, mn, Bmn, sqr, Bsqr, rs, Brs)
            for h in range(4):
                bk = psb()
                for c in range(KC):
                    mm(bk, PS[bk][:, 0:MEM], wm[:, c, h * 128:(h + 1) * 128], mn[:, c, :], c == 0, c == KC - 1, [Bwm, Bmn])
                rstd1(bk, MEM, 1.0 / 128, sq1, Bsq1, rs, Brs)
                stt(mkn[:, h, :], PS[bk][:, 0:MEM], pp[:, 55:56], rs[:, 0:MEM], ALU.mult, ALU.mult, [PSB[bk], Brs, Bpp], [Bmkn])
            for j in range(2):
                bk = psb()
                for c in range(KC):
                    mm(bk, PS[bk], mn[:, c, j * 128:(j + 1) * 128], wm[:, c, 512:1024], c == 0, c == KC - 1, [Bwm, Bmn])
                act(mv[:, j, :], PS[bk], AF.Copy, [PSB[bk]], [Bmv])
            P.barrier()

            off[0] = persist_mark
            wA = carve(KC * WA_COLS, BF16).rearrange("p (c n) -> p c n", c=KC)
            BwA = Buf("wA")
            wait_weight(l, "in")
            for c in range(KC):
                P.dma("sp", wA[:, c, :], b_in[l][c * 128:(c + 1) * 128, 0:WA_COLS], writes=[BwA])
            xt = carve(KC * TT).rearrange("p (c n) -> p c n", c=KC); Bxt = Buf("xt")
            sqr = [carve(512, BF16) for _ in range(2)]; Bsqr = [Buf("sqr0"), Buf("sqr1")]
            hT2 = [carve(KC * TT, BF16).rearrange("p (c n) -> p c n", c=KC) for _ in range(2)]; BhT2 = [Buf("hT0"), Buf("hT1")]
            rs = carve(512); Brs = Buf("rs")
            rs2 = carve(512); Brs2 = Buf("rs2")
            sq1 = carve(512, BF16); Bsq1 = Buf("sq1")
            qT = [carve(512) for _ in range(4)]; BqT = [Buf(f"qT{h}") for h in range(4)]
            kT = [carve(512) for _ in range(4)]; BkT = [Buf(f"kT{h}") for h in range(4)]
            a1T = carve(512, BF16); Ba1 = Buf("a1T")
            ktm = carve(4 * 256).rearrange("p (j n) -> p j n", j=4); Bktm = Buf("ktm")
            vtm = carve(4 * 512, BF16).rearrange("p (j n) -> p j n", j=4); Bvtm = Buf("vtm")
            la = carve(4 * 256).rearrange("p (j n) -> p j n", j=4); Bla = Buf("la")
            t1j = [carve(256) for _ in range(4)]; t2j = [carve(256) for _ in range(4)]
            Bt1j = [Buf(f"t1_{j}") for j in range(4)]; Bt2j = [Buf(f"t2_{j}") for j in range(4)]
            ektm = carve(256); Bektm = Buf("ektm")
            kintm = carve(4 * 256, BF16).rearrange("p (j n) -> p j n", j=4); Bkintm = Buf("kintm")
            sgg = [carve(512) for _ in range(4)]; Bsgg = [Buf(f"sgg{h}") for h in range(4)]
            eg = carve(512); Beg = Buf("eg")
            eqh = [carve(512) for _ in range(4)]; Beqh = [Buf(f"eq{h}") for h in range(4)]
            ekT = carve(512); BekT = Buf("ekT")
            qinh = [carve(512, BF16) for _ in range(4)]; Bqinh = [Buf(f"qin{h}") for h in range(4)]
            kinh = [carve(512, BF16) for _ in range(4)]; Bkinh = [Buf(f"kin{h}") for h in range(4)]
            atbh = [[carve(128, BF16) for _ in range(2)] for _ in range(4)]
            Batbh = [[Buf(f"atb{h}_{i}") for i in range(2)] for h in range(4)]
            csth = [carve(128) for _ in range(4)]; Bcsh = [Buf(f"cs{h}") for h in range(4)]
            y1 = carve(512); By1 = Buf("y1")
            ystage = carve(4 * 512, BF16).rearrange("p (h n) -> p h n", h=4); Bys = Buf("ys")
            qstage = carve(4 * 512, BF16).rearrange("p (h n) -> p h n", h=4); Bqs = Buf("qs")
            kstage = carve(4 * 512, BF16).rearrange("p (h n) -> p h n", h=4); Bks = Buf("ks")
            vstage = carve(4 * 512, BF16).rearrange("p (j n) -> p j n", j=4); Bvs = Buf("vs")
            lf1 = carve(16); lf2 = carve(16); Blf = Buf("lf")

            P.dma("sp", xt, xTv[:, :, 0:TT], writes=[Bxt])
            ring["lo"], ring["hi"], ring["i"] = 0, 8, 0
            norm_tile(xt, Bxt, 0, TT, hT2[0], BhT2[0], sqr, Bsqr, rs, Brs)
            if NT > 1:
                P.dma("sp", xt, xTv[:, :, TT:2 * TT], writes=[Bxt])
            for t in range(NT):
                hT, BhT = hT2[t % 2], BhT2[t % 2]

                def proj_fm(col0, M):
                    bk = psb()
                    for c in range(KC):
                        mm(bk, PS[bk][0:M, :], wA[:, c, col0:col0 + M], hT[:, c, :], c == 0, c == KC - 1, [BwA, BhT])
                    return bk

                def proj_tm(col0, N, j):
                    bk = psb()
                    for c in range(KC):
                        mm(bk, PS[bk][:, 0:N], hT[:, c, j * 128:(j + 1) * 128], wA[:, c, col0:col0 + N], c == 0, c == KC - 1, [BwA, BhT])
                    return bk

                bk = proj_fm(O_GA, 16)
                act(a1T[0:16, :], PS[bk][0:16, :], AF.Copy, [PSB[bk]], [Ba1])
                for h in range(2):
                    bk = proj_fm(O_GQ + h * 64, 64)
                    act(qT[h][0:64, :], PS[bk][0:64, :], AF.Copy, [PSB[bk]], [BqT[h]], scale=0.125)
                    bk = proj_fm(O_GK + h * 64, 64)
                    cp("dve", kT[h][0:64, :], PS[bk][0:64, :], [PSB[bk]], [BkT[h]])
                for j in range(4):
                    bk = psb()
                    mm(bk, PS[bk][:, 0:256], a1T[0:16, j * 128:(j + 1) * 128], wa2b[0:16, :], True, True, [Ba1, Bwa2])
                    tt("dve", t1j[j], PS[bk][:, 0:256], pb[:, 0:256], ALU.add, [PSB[bk], Bpp], [Bt1j[j]])
                    stt(t2j[j], t1j[j], -1.0, t1j[j], ALU.mult, ALU.max, [Bt1j[j]], [Bt2j[j]], eng="dve")
                    act(t2j[j], t2j[j], AF.Exp, [Bt2j[j]], [Bt2j[j]], scale=-1.0)
                    act(t2j[j], t2j[j], AF.Ln, [Bt2j[j]], [Bt2j[j]], bias=1.0)
                    stt(la[:, j, :], t1j[j], 0.0, t2j[j], ALU.min, ALU.subtract, [Bt1j[j], Bt2j[j]], [Bla])
                for h in range(2, 4):
                    bk = proj_fm(O_GQ + h * 64, 64)
                    act(qT[h][0:64, :], PS[bk][0:64, :], AF.Copy, [PSB[bk]], [BqT[h]], scale=0.125)
                    bk = proj_fm(O_GK + h * 64, 64)
                    cp("dve", kT[h][0:64, :], PS[bk][0:64, :], [PSB[bk]], [BkT[h]])
                for h in range(4):
                    bk = proj_fm(O_GG + h * 128, 128)
                    act(eg, PS[bk], AF.Exp, [PSB[bk]], [Beg], scale=-1.0)
                    act(eg, eg, AF.Ln, [Beg], [Beg], bias=1.0)
                    act(eg, eg, AF.Exp, [Beg], [Beg], scale=-1.0)
                    tt("dve", sgg[h], PS[bk], eg, ALU.mult, [PSB[bk], Beg], [Bsgg[h]])
                for j in range(4):
                    bk = proj_tm(O_GK, 256, j)
                    cp("dve", ktm[:, j, :], PS[bk][:, 0:256], [PSB[bk]], [Bktm])
                    bk = proj_tm(O_GV, 512, j)
                    act(vtm[:, j, :], PS[bk], AF.Copy, [PSB[bk]], [Bvtm])
                pend = None
                for (o_col, gcol, stage, Bst) in ((O_FQ, 52, qstage, Bqs), (O_FK, 53, kstage, Bks)):
                    for h in range(4):
                        bk = proj_fm(o_col + h * 128, 128)
                        if pend is not None:
                            pend()

                        def pend(bk=bk, stage=stage, Bst=Bst, gcol=gcol, h=h):
                            rstd1(bk, TT, 1.0 / 128, sq1, Bsq1, rs2, Brs2)
                            stt(stage[:, h, :], PS[bk], pp[:, gcol:gcol + 1], rs2, ALU.mult, ALU.mult, [PSB[bk], Brs2, Bpp], [Bst])
                for j in range(4):
                    bk = proj_tm(O_FV, 512, j)
                    if j == 0:
                        pend()
                    if j % 2 == 0:
                        act(vstage[:, j, :], PS[bk], AF.Copy, [PSB[bk]], [Bvs])
                    else:
                        cp("dve", vstage[:, j, :], PS[bk], [PSB[bk]], [Bvs])
                P.dma("sp", foxq_d[:, :, t * TT:(t + 1) * TT].rearrange("h p s -> p h s"), qstage, reads=[Bqs])
                P.dma("sp", foxk_d[:, :, t * TT:(t + 1) * TT].rearrange("h p s -> p h s"), kstage, reads=[Bks])
                P.dma("sp", foxv_d[t * TT:(t + 1) * TT, :].rearrange("(j p) n -> p j n", p=128), vstage, reads=[Bvs])
                bk = psb()
                for j in range(4):
                    for c in range(KC):
                        mm(bk, PS[bk][:, j * 4:(j + 1) * 4], hT[:, c, j * 128:(j + 1) * 128], wA[:, c, O_FF:O_FF + 4], c == 0, c == KC - 1, [BwA, BhT])
                lfo = logf[:, t * 16:(t + 1) * 16]
                P.op("dve", lambda e, bk=bk, lf1=lf1: e.tensor_tensor(out=lf1.rearrange("p (j h) -> p j h", h=4), in0=PS[bk][:, 0:16].rearrange("p (j h) -> p j h", h=4),
                                                                      in1=pb[:, 256:260].unsqueeze(1).broadcast_to([128, 4, 4]), op=ALU.add),
                     reads=[PSB[bk], Bpp], writes=[Blf])
                stt(lf2, lf1, -1.0, lf1, ALU.mult, ALU.max, [Blf], [Blf])
                act(lf2, lf2, AF.Exp, [Blf], [Blf], scale=-1.0)
                act(lf2, lf2, AF.Ln, [Blf], [Blf], bias=1.0)
                stt(lfo, lf1, 0.0, lf2, ALU.min, ALU.subtract, [Blf], [Blogf])
                for j in range(4):
                    bk = psb()
                    mm(bk, PS[bk][:, 0:256], tri16, la[:, j, :], True, True, [Bcst, Bla])
                    act(ektm, PS[bk][:, 0:256], AF.Exp, [PSB[bk]], [Bektm], scale=-1.0)
                    tt("pool", kintm[:, j, :], ktm[:, j, :], ektm, ALU.mult, [Bktm, Bektm], [Bkintm])
                ring["lo"], ring["hi"], ring["i"] = 0, 4, 0
                for h in range(4):
                    bkc = psb()
                    for j in range(4):
                        mm(bkc, PS[bkc][0:64, j * 128:(j + 1) * 128], la[:, j, h * 64:(h + 1) * 64], tri16, True, True, [Bla, Bcst])
                    act(eqh[h][0:64, :], PS[bkc][0:64, :], AF.Exp, [PSB[bkc]], [Beqh[h]])
                    act(ekT[0:64, :], PS[bkc][0:64, :], AF.Exp, [PSB[bkc]], [BekT], scale=-1.0)
                    tt("pool", qinh[h][0:64, :], qT[h][0:64, :], eqh[h][0:64, :], ALU.mult, [BqT[h], Beqh[h]], [Bqinh[h]])
                    tt("pool", kinh[h][0:64, :], kT[h][0:64, :], ekT[0:64, :], ALU.mult, [BkT[h], BekT], [Bkinh[h]])
                if t + 1 < NT:
                    norm_tile(xt, Bxt, 0, TT, hT2[(t + 1) % 2], BhT2[(t + 1) % 2], sqr, Bsqr, rs, Brs)
                    if t + 2 < NT:
                        P.dma("sp", xt, xTv[:, :, (t + 2) * TT:(t + 3) * TT], writes=[Bxt])
                for j in range(4):
                    js = slice(j * 128, (j + 1) * 128)
                    for h in range(4):
                        bka = psb()
                        mm(bka, PS[bka][:, 0:128], kinh[h][0:64, js], qinh[h][0:64, js], True, True, [Bkinh[h], Bqinh[h]])
                        ab, Bab = atbh[h][j % 2], Batbh[h][j % 2]
                        tt("dve", ab, PS[bka][:, 0:128], tri, ALU.mult, [PSB[bka], Bcst], [Bab])
                    for h in range(4):
                        bko = 4 + h
                        ab, Bab = atbh[h][j % 2], Batbh[h][j % 2]
                        mm(bko, PS[bko][:, js], vtm[:, j, h * 128:(h + 1) * 128], ab, True, False, [Bvtm, Bab])
                        mm(bko, PS[bko][:, js], Sbf[h][0:64, :], qinh[h][0:64, js], False, True, [BSb[h], Bqinh[h]])
                    for h in range(4):
                        bks = psb()
                        mm(bks, PS[bks][0:64, 0:128], kintm[:, j, h * 64:(h + 1) * 64], vtm[:, j, h * 128:(h + 1) * 128], True, True, [Bkintm, Bvtm])
                        dcol = eqh[h][0:64, j * 128 + 127:j * 128 + 128]
                        act(csth[h][0:64, :], PS[bks][0:64, 0:128], AF.Identity, [PSB[bks], Beqh[h]], [Bcsh[h]], scale=dcol)
                        stt(Sst[h][0:64, :], Sst[h][0:64, :], dcol, csth[h][0:64, :], ALU.mult, ALU.add, [BS[h], Beqh[h], Bcsh[h]], [BS[h]])
                        cp("pool", Sbf[h][0:64, :], Sst[h][0:64, :], [BS[h]], [BSb[h]])
                for h in range(4):
                    bko = 4 + h
                    rstd1(bko, TT, 1.0 / 128, sq1, Bsq1, rs2, Brs2)
                    stt(y1, PS[bko], pp[:, 48 + h:49 + h], rs2, ALU.mult, ALU.mult, [PSB[bko], Brs2, Bpp], [By1])
                    tt("pool", ystage[:, h, :], y1, sgg[h], ALU.mult, [By1, Bsgg[h]], [Bys])
                ring["lo"], ring["hi"], ring["i"] = 0, 8, 0
                P.dma("sp", yT_d[0:512, t * TT:(t + 1) * TT].rearrange("(h p) s -> p h s", p=128), ystage, reads=[Bys])
            ring["lo"], ring["hi"], ring["i"] = 0, 8, 0
            NBH = NB * 4
            bk1 = psb()
            mm(bk1, PS[bk1][:, 0:NBH], tri, logf, True, True, [Bcst, Blogf])
            bk2 = psb()
            mm(bk2, PS[bk2][:, 0:NBH], ones_f, logf, True, True, [Bones, Blogf])
            bsum = carve(NBH); Bbs = Buf("bsum")
            incl = carve(NBH); Bincl = Buf("incl")
            cp("dve", bsum, PS[bk2][:, 0:NBH], [PSB[bk2]], [Bbs])
            onesrow = carve(NB); Bor = Buf("onesrow")
            P.op("dve", lambda e, onesrow=onesrow: e.memset(onesrow, 1.0), writes=[Bor])
            for h in range(4):
                bv = bsum.rearrange("p (b h) -> p h b", h=4)[:, h, :]
                iv = incl.rearrange("p (b h) -> p h b", h=4)[:, h, :]
                P.op("dve", lambda e, bv=bv, iv=iv, onesrow=onesrow: e.tensor_tensor_scan(out=iv, data0=onesrow, data1=bv, initial=0.0, op0=ALU.mult, op1=ALU.add),
                     reads=[Bbs, Bor], writes=[Bincl])
            tt("dve", incl, incl, bsum, ALU.subtract, [Bincl, Bbs], [Bincl])
            tt("dve", cum_all, PS[bk1][:, 0:NBH], incl, ALU.add, [PSB[bk1], Bincl], [Bcum])
            P.op("dve", lambda e: e.tensor_scalar_mul(out=negcum, in0=cum_all, scalar1=-1.0), reads=[Bcum], writes=[Bcum])
            if dbg:
                P.dma("sp", dbg_d["cum"], cum_all, reads=[Bcum])
            P.barrier()
            if dbg:
                for nm, src_ in (("foxq", foxq_d), ("foxk", foxk_d), ("foxv", foxv_d)):
                    P.dma("sp", dbg_d[nm], src_)
                P.barrier()

            off[0] = persist_mark
            kres2 = [carve(S, BF16) for _ in range(2)]; Bkres2 = [Buf("kres0"), Buf("kres1")]
            vres2 = [carve(NB * 128, BF16).rearrange("p (b d) -> p b d", d=128) for _ in range(2)]; Bvres2 = [Buf("vres0"), Buf("vres1")]
            qt2 = [carve(512, BF16) for _ in range(2)]; Bqt2 = [Buf("q0"), Buf("q1")]
            cq2 = [carve(512) for _ in range(2)]; Bcq2 = [Buf("cq0"), Buf("cq1")]
            dg = [carve(128) for _ in range(2)]; Bdg = [Buf("dg0"), Buf("dg1")]
            NRB = 6
            tb = [carve(512) for _ in range(NRB)]; Btb = [Buf(f"tb{i}") for i in range(NRB)]
            pt = [carve(512, BF16) for _ in range(NRB)]; Bpt = [Buf(f"pt{i}") for i in range(NRB)]
            rl = carve(512); Brl = Buf("rl")
            yst = [carve(512, BF16) for _ in range(2)]; Byst = [Buf("yst0"), Buf("yst1")]
            ring["lo"], ring["hi"], ring["i"] = 4, 8, 0
            scale = 128.0 ** -0.5
            LA = 4

            def load_kv(h):
                P.dma("sp", kres2[h % 2], foxk_d[h], writes=[Bkres2[h % 2]])
                vsrc = foxv_d[:, h * 128:(h + 1) * 128].rearrange("(b p) d -> p b d", p=128)
                step = 16
                for b0_ in range(0, NB, step):
                    b1_ = min(NB, b0_ + step)
                    P.dma("sp", vres2[h % 2][:, b0_:b1_, :], vsrc[:, b0_:b1_, :], writes=[Bvres2[h % 2]])

            iters = [(h, qt) for h in range(4) for qt in range(NT)]

            def load_q(i):
                h, qt = iters[i]
                P.dma("sp", qt2[i % 2], foxq_d[h][:, qt * TT:(qt + 1) * TT], writes=[Bqt2[i % 2]])

            def make_cq(i):
                h, qt = iters[i]
                cq, Bcq = cq2[i % 2], Bcq2[i % 2]
                bkq = psb()
                for j in range(4):
                    blk = qt * 4 + j
                    d_, Bd_ = dg[j % 2], Bdg[j % 2]
                    P.op("dve", lambda e, d_=d_, blk=blk, h=h: e.tensor_scalar_mul(out=d_, in0=ident, scalar1=cum_all[:, blk * 4 + h:blk * 4 + h + 1]),
                         reads=[Bcst, Bcum], writes=[Bd_])
                    mm(bkq, PS[bkq][:, j * 128:(j + 1) * 128], ones_f, d_, True, True, [Bones, Bd_])
                act(cq, PS[bkq], AF.Copy, [PSB[bkq]], [Bcq])

            steps = []
            for i, (h, qt) in enumerate(iters):
                for kb in range(qt * 4 + 4):
                    steps.append((i, h, qt, kb))
            sbank = {}

            def stage1(n):
                i, h, qt, kb = steps[n]
                par = i % 2
                if kb == 0:
                    if i + 1 < len(iters):
                        load_q(i + 1)
                        make_cq(i + 1)
                kres, Bkres = kres2[h % 2], Bkres2[h % 2]
                qtile, Bq = qt2[par], Bqt2[par]
                cq, Bcq = cq2[par], Bcq2[par]
                j = kb - qt * 4
                c0 = max(0, j) * 128
                cs_ = slice(c0, TT)
                bks = psb()
                mm(bks, PS[bks][:, cs_], kres[:, kb * 128:(kb + 1) * 128], qtile[:, cs_], True, True, [Bkres, Bq])
                tbuf, Btbuf = tb[n % NRB], Btb[n % NRB]
                pbuf, Bpbuf = pt[n % NRB], Bpt[n % NRB]
                stt(tbuf[:, cs_], PS[bks][:, cs_], scale, cq[:, cs_], ALU.mult, ALU.add, [PSB[bks], Bcq], [Btbuf])
                if j >= 0:
                    tt("pool", tbuf[:, c0:c0 + 128], tbuf[:, c0:c0 + 128], maskneg, ALU.add, [Btbuf, Bcst], [Btbuf])
                act(pbuf[:, cs_], tbuf[:, cs_], AF.Exp, [Btbuf, Bcum], [Bpbuf], bias=negcum[:, kb * 4 + h:kb * 4 + h + 1])

            def stage2(n):
                i, h, qt, kb = steps[n]
                par = i % 2
                vres, Bvres = vres2[h % 2], Bvres2[h % 2]
                j = kb - qt * 4
                c0 = max(0, j) * 128
                cs_ = slice(c0, TT)
                nkb = qt * 4 + 4
                pbuf, Bpbuf = pt[n % NRB], Bpt[n % NRB]
                bo, bl = (0, 1) if par == 0 else (2, 3)
                if kb == 0 and qt == 0 and h + 1 < 4:
                    load_kv(h + 1)
                mm(bo, PS[bo][:, cs_], vres[:, kb, :], pbuf[:, cs_], kb == 0, kb == nkb - 1, [Bvres, Bpbuf])
                mm(bl, PS[bl][:, cs_], ones_b, pbuf[:, cs_], kb == 0, kb == nkb - 1, [Bones, Bpbuf])
                if kb == nkb - 1:
                    act(rl, PS[bl], AF.Copy, [PSB[bl]], [Brl])
                    P.op("dve", lambda e, rl=rl: e.reciprocal(out=rl, in_=rl), reads=[Brl], writes=[Brl])
                    ys, Bys_ = yst[par], Byst[par]
                    tt("dve", ys, PS[bo], rl, ALU.mult, [PSB[bo], Brl], [Bys_])
                    P.dma("sp", yT_d[512 + h * 128:512 + (h + 1) * 128, qt * TT:(qt + 1) * TT], ys, reads=[Bys_])

            load_kv(0)
            load_q(0)
            make_cq(0)
            for n in range(len(steps) + LA):
                if n < len(steps):
                    stage1(n)
                if n >= LA:
                    stage2(n - LA)
            ring["lo"], ring["hi"], ring["i"] = 0, 8, 0
            P.barrier()
            if dbg:
                P.dma("sp", dbg_d["yT"], yT_d)
                P.barrier()

            off[0] = persist_mark
            wC = carve(KC * WC_COLS, BF16).rearrange("p (c n) -> p c n", c=KC); BwC = Buf("wC")
            wB = carve(12 * D, BF16).rearrange("p (c n) -> p c n", c=12); BwB = Buf("wB")
            wO = carve(KC * D, BF16).rearrange("p (c n) -> p c n", c=KC); BwO = Buf("wO")
            wait_weight(l, "in"); wait_weight(l, "br"); wait_weight(l, "out")
            for c in range(KC):
                P.dma("sp", wC[:, c, :], b_in[l][c * 128:(c + 1) * 128, WA_COLS:INW], writes=[BwC])
            for c in range(12):
                P.dma("sp", wB[:, c, :], b_br[l][c * 128:(c + 1) * 128, :], writes=[BwB])
            for c in range(KC):
                P.dma("sp", wO[:, c, :], b_out[l][c * 128:(c + 1) * 128, :], writes=[BwO])
            xt = carve(KC * TT).rearrange("p (c n) -> p c n", c=KC); Bxt = Buf("xt")
            sqr = [carve(512, BF16) for _ in range(2)]; Bsqr = [Buf("sqr0"), Buf("sqr1")]
            hT = carve(KC * TT, BF16).rearrange("p (c n) -> p c n", c=KC); BhT = Buf("hT")
            rs = carve(512); Brs = Buf("rs")
            rs2 = carve(512); Brs2 = Buf("rs2")
            sq1 = carve(512, BF16); Bsq1 = Buf("sq1")
            yt = carve(12 * TT, BF16).rearrange("p (c n) -> p c n", c=12)
            Byt = Buf("yt"); Bym = Buf("ym")
            mqn = carve(512, BF16); Bmqn = Buf("mqn")
            ptm = [carve(512, BF16) for _ in range(2)]; Bptm = [Buf("ptm0"), Buf("ptm1")]
            rl = carve(512); Brl = Buf("rl")
            gate = [carve(512) for _ in range(3)]; Bgate = [Buf(f"gate{i}") for i in range(3)]
            term = [carve(512) for _ in range(3)]; Bterm = [Buf(f"term{i}") for i in range(3)]
            mg = carve(KC * TT, BF16).rearrange("p (c n) -> p c n", c=KC); Bmg = [Buf(f"mg{c}") for c in range(KC)]
            yTv = yT_d.rearrange("(c p) s -> p c s", p=128)

            for t in range(NT):
                P.dma("sp", xt, xTv[:, :, t * TT:(t + 1) * TT], writes=[Bxt])
                P.dma("sp", yt[:, 0:8, :], yTv[:, :, t * TT:(t + 1) * TT], writes=[Byt])
                norm_tile(xt, Bxt, 0, TT, hT, BhT, sqr, Bsqr, rs, Brs)
                for h in range(4):
                    bk = psb()
                    for c in range(KC):
                        mm(bk, PS[bk], wC[:, c, h * 128:(h + 1) * 128], hT[:, c, :], c == 0, c == KC - 1, [BwC, BhT])
                    rstd1(bk, TT, 1.0 / 128, sq1, Bsq1, rs2, Brs2)
                    stt(mqn, PS[bk], pp[:, 54:55], rs2, ALU.mult, ALU.mult, [PSB[bk], Brs2, Bpp], [Bmqn])
                    bo = psb(); bl = psb()
                    for mb in range(2):
                        bks = psb()
                        mm(bks, PS[bks], mkn[:, h, mb * 128:(mb + 1) * 128], mqn, True, True, [Bmkn, Bmqn])
                        act(ptm[mb], PS[bks], AF.Exp, [PSB[bks]], [Bptm[mb]], scale=128.0 ** -0.5)
                        mm(bo, PS[bo], mv[:, mb, h * 128:(h + 1) * 128], ptm[mb], mb == 0, mb == 1, [Bmv, Bptm[mb]])
                        mm(bl, PS[bl], ones_b, ptm[mb], mb == 0, mb == 1, [Bones, Bptm[mb]])
                    act(rl, PS[bl], AF.Copy, [PSB[bl]], [Brl])
                    P.op("dve", lambda e, rl=rl: e.reciprocal(out=rl, in_=rl), reads=[Brl], writes=[Brl])
                    tt("dve", yt[:, 8 + h, :], PS[bo], rl, ALU.mult, [PSB[bo], Brl], [Bym])
                for cb in range(KC):
                    for i in range(3):
                        bkb = psb()
                        for kc in range(4):
                            mm(bkb, PS[bkb], wB[:, i * 4 + kc, cb * 128:(cb + 1) * 128], yt[:, i * 4 + kc, :], kc == 0, kc == 3, [BwB, Byt, Bym])
                        bkg = psb()
                        gc0 = 512 + i * D + cb * 128
                        for c in range(KC):
                            mm(bkg, PS[bkg], wC[:, c, gc0:gc0 + 128], hT[:, c, :], c == 0, c == KC - 1, [BwC, BhT])
                        act(gate[i], PS[bkg], AF.Sigmoid, [PSB[bkg], Bpp], [Bgate[i]], bias=pp[:, 24 + i * 8 + cb:25 + i * 8 + cb])
                        tt("dve", term[i], PS[bkb], gate[i], ALU.mult, [PSB[bkb], Bgate[i]], [Bterm[i]])
                    tt("pool", term[0], term[0], term[1], ALU.add, [Bterm[0], Bterm[1]], [Bterm[0]])
                    tt("pool", mg[:, cb, :], term[0], term[2], ALU.add, [Bterm[0], Bterm[2]], [Bmg[cb]])
                for cb in range(KC):
                    bk = psb()
                    for kc in range(KC):
                        mm(bk, PS[bk], wO[:, kc, cb * 128:(cb + 1) * 128], mg[:, kc, :], kc == 0, kc == KC - 1, [BwO] + Bmg)
                    tt("dve", xt[:, cb, :], xt[:, cb, :], PS[bk], ALU.add, [Bxt, PSB[bk]], [Bxt])
                P.dma("sp", xTv[:, :, t * TT:(t + 1) * TT], xt, reads=[Bxt])
                if dbg:
                    P.dma("sp", dbg_d["xmid"].rearrange("(c p) s -> p c s", p=128)[:, :, t * TT:(t + 1) * TT], xt, reads=[Bxt])
            P.barrier()

            off[0] = persist_mark
            wG = carve(KC * DFF, BF16).rearrange("p (c n) -> p c n", c=KC); BwG = Buf("wG")
            wU = carve(KC * DFF, BF16).rearrange("p (c n) -> p c n", c=KC); BwU = Buf("wU")
            wait_weight(l, "fg"); wait_weight(l, "fu"); wait_weight(l, "fd")
            for c in range(KC):
                P.dma("sp", wG[:, c, :], b_fg[l][c * 128:(c + 1) * 128, :], writes=[BwG])
                P.dma("sp", wU[:, c, :], b_fu[l][c * 128:(c + 1) * 128, :], writes=[BwU])
            wD2 = [carve(NF * 128, BF16).rearrange("p (f n) -> p f n", f=NF) for _ in range(2)]
            BwD2 = [Buf("wD0"), Buf("wD1")]
            xt = carve(KC * TT).rearrange("p (c n) -> p c n", c=KC); Bxt = Buf("xt")
            sqr = [carve(512, BF16) for _ in range(2)]; Bsqr = [Buf("sqr0"), Buf("sqr1")]
            hT = carve(KC * TT, BF16).rearrange("p (c n) -> p c n", c=KC); BhT = Buf("hT")
            rs = carve(512); Brs = Buf("rs")
            aT = carve(NF * TT, BF16).rearrange("p (f n) -> p f n", f=NF); BaT = [Buf(f"aT{f}") for f in range(NF)]
            sg = [carve(512) for _ in range(2)]; Bsg = [Buf("sg0"), Buf("sg1")]
            ost = carve(4 * D).rearrange("p (j n) -> p j n", j=4) if last_layer else None
            Bost = Buf("ost")
            b_fdv = b_fd[l].rearrange("(f p) n -> p f n", p=128)

            wdi = 0
            for t in range(NT):
                P.dma("sp", xt, xTv[:, :, t * TT:(t + 1) * TT], writes=[Bxt])
                norm_tile(xt, Bxt, 8, TT, hT, BhT, sqr, Bsqr, rs, Brs)
                for f in range(NF):
                    bg = psb()
                    for c in range(KC):
                        mm(bg, PS[bg], wG[:, c, f * 128:(f + 1) * 128], hT[:, c, :], c == 0, c == KC - 1, [BwG, BhT])
                    bu = psb()
                    for c in range(KC):
                        mm(bu, PS[bu], wU[:, c, f * 128:(f + 1) * 128], hT[:, c, :], c == 0, c == KC - 1, [BwU, BhT])
                    act(sg[f % 2], PS[bg], AF.Silu, [PSB[bg]], [Bsg[f % 2]])
                    tt("dve", aT[:, f, :], PS[bu], sg[f % 2], ALU.mult, [PSB[bu], Bsg[f % 2]], [BaT[f]])
                for cb in range(KC):
                    wD, BwD = wD2[wdi % 2], BwD2[wdi % 2]
                    wdi += 1
                    P.dma("sp", wD, b_fdv[:, :, cb * 128:(cb + 1) * 128], writes=[BwD])
                    bk = psb()
                    for f in range(NF):
                        mm(bk, PS[bk], wD[:, f, :], aT[:, f, :], f == 0, f == NF - 1, [BwD] + BaT)
                    tt("dve", xt[:, cb, :], xt[:, cb, :], PS[bk], ALU.add, [Bxt, PSB[bk]], [Bxt])
                if not last_layer:
                    P.dma("sp", xTv[:, :, t * TT:(t + 1) * TT], xt, reads=[Bxt])
                else:
                    for j in range(4):
                        for half in range(2):
                            bk = psb()
                            for q in range(4):
                                cb = half * 4 + q
                                P.op("pe", lambda e, bk=bk, q=q, cb=cb, j=j, xt=xt: e.transpose(PS[bk][:, q * 128:(q + 1) * 128], xt[:, cb, j * 128:(j + 1) * 128], ident),
                                     reads=[Bxt, Bcst], writes=[PSB[bk]])
                            if half == 0:
                                act(ost[:, j, 0:512], PS[bk], AF.Copy, [PSB[bk]], [Bost])
                            else:
                                cp("dve", ost[:, j, 512:1024], PS[bk], [PSB[bk]], [Bost])
                    P.dma("sp", out_d[t * TT:(t + 1) * TT, :].rearrange("(j p) n -> p j n", p=128), ost, reads=[Bost])
            P.barrier()
        P.finalize()
    return nc


def host_consts():
    ident = np.eye(128, dtype=np.float32)
    s = np.arange(128)[:, None]
    c = np.arange(128)[None, :]
    tri = (s <= c).astype(np.float32)
    return np.ascontiguousarray(np.concatenate([ident, tri, tri / 16.0, (tri - 1.0) * (-NEG)], axis=1))


def host_params(inp, depth):
    pp = np.zeros((depth, 128, NP_), np.float32)
    pb = np.zeros((depth, 128, 260), np.float32)
    for l in range(depth):
        pp[l, :, 0:8] = inp["g_mix"][l].reshape(8, 128).T
        pp[l, :, 8:16] = inp["g_ffn"][l].reshape(8, 128).T
        pp[l, :, 16:24] = inp["g_mem"][l].reshape(8, 128).T
        pp[l, :, 24:48] = inp["b_gate"][l].reshape(24, 128).T
        pp[l, :, 48:52] = inp["g_gla_out"][l].reshape(4, 128).T
        pp[l, :, 52] = inp["g_fox_q"][l]
        pp[l, :, 53] = inp["g_fox_k"][l]
        pp[l, :, 54] = inp["g_mem_q"][l]
        pp[l, :, 55] = inp["g_mem_k"][l]
        pb[l, :, 0:256] = inp["b_gla_a"][l][None, :]
        pb[l, :, 256:260] = inp["b_fox_f"][l][None, :]
    return pp, pb


def make_in_maps(inp, S, depth, ncores):
    inp = {k: np.asarray(v) for k, v in inp.items()}
    pp, pb = host_params(inp, depth)
    consts = host_consts()
    B = inp["x"].shape[0]
    shared = dict(
        w_in=np.ascontiguousarray(inp["w_in"][:depth]), w_mem_kv=np.ascontiguousarray(inp["w_mem_kv"][:depth]),
        w_branch=np.ascontiguousarray(inp["w_branch"][:depth].reshape(depth, 1536, D)),
        w_out=np.ascontiguousarray(inp["w_out"][:depth]), w_ffn_gate=np.ascontiguousarray(inp["w_ffn_gate"][:depth]),
        w_ffn_up=np.ascontiguousarray(inp["w_ffn_up"][:depth]), w_ffn_down=np.ascontiguousarray(inp["w_ffn_down"][:depth]),
        w_gla_a2=np.ascontiguousarray(inp["w_gla_a2"][:depth]), pp=pp, pb=pb, consts=consts)
    maps = []
    for cid in range(ncores):
        b = cid % B
        m = dict(shared)
        m["x"] = np.ascontiguousarray(inp["x"][b, :S])
        m["mem"] = np.ascontiguousarray(inp["mem"][b])
        maps.append(m)
    return maps


_NC_CACHE = {}


def kernel(**inputs):
    S, depth = 8192, 4
    key = (S, depth)
    if key not in _NC_CACHE:
        _NC_CACHE[key] = build_program(S, depth)
    nc = _NC_CACHE[key]
    maps = make_in_maps(inputs, S, depth, 8)
    res = run_bass_kernel_spmd(nc, maps, core_ids=list(range(8)))
    B = np.asarray(inputs["x"]).shape[0]
    out = np.stack([np.asarray(res.results[b]["out"]) for b in range(B)], axis=0)
    return out.astype(np.float32)
```
